# Optimizing a Trainium2 kernel written in Bass

```python
import jax, jax.numpy as jnp
from jax import lax
import numpy as np

D_MODEL = 1024
BATCH = 8
SEQ = 2048
DEPTH = 4

GRID_W = 64
CTX_LEN = 256
A_HEADS = 4
A_DK = 128
A_DV = 128
A_KW = A_HEADS * A_DK
A_VW = A_HEADS * A_DV
A_CHUNK = 16
B_WIDTH = 1024
B_HEADDIM = 64
B_HEADS = B_WIDTH // B_HEADDIM
B_GROUPS = 4
B_HPG = B_HEADS // B_GROUPS
B_STATE = 128
B_CONV = 5
B_CONV_CH = B_WIDTH + 2 * B_GROUPS * B_STATE
B_CHUNK = 128
IN_SIZES = (A_KW, A_KW, A_KW, A_VW, A_VW, B_WIDTH, B_CONV_CH, B_HEADS, B_HEADS, D_MODEL, D_MODEL)
IN_COLS = sum(IN_SIZES)
IN_SPLITS = tuple(int(s) for s in np.cumsum(IN_SIZES)[:-1])
DN_ALPHA = (2 * DEPTH) ** 0.25
DN_BETA = (8 * DEPTH) ** -0.25
LN_EPS = 1e-5
RMS_EPS = 1e-6
F_FLOOR = 1e-30

kernel_name = "hybrid_hgrn2_ssd_prefix_dit"


def layer_norm(x, g, b):
    xf = x.astype(jnp.float32)
    mu = jnp.mean(xf, -1, keepdims=True)
    var = jnp.mean(jnp.square(xf - mu), -1, keepdims=True)
    return (xf - mu) * lax.rsqrt(var + LN_EPS) * g + b


def group_rms(u, groups):
    u = u.astype(jnp.float32)
    sh = u.shape
    u = u.reshape(*sh[:-1], groups, sh[-1] // groups)
    u = u * lax.rsqrt(jnp.mean(u * u, -1, keepdims=True) + RMS_EPS)
    return u.reshape(sh)


def masked_exp(diff, mask):
    return jnp.where(mask, jnp.exp(jnp.where(mask, diff, 0.0)), 0.0)


def dwconv_centred(u, w, b, n_rows):
    bsz, L, C = u.shape
    v = u.reshape(bsz * n_rows, L // n_rows, C)
    pad = B_CONV // 2
    out = lax.conv_general_dilated(v, w[:, None, :].astype(v.dtype), (1,), [(pad, pad)],
                                   dimension_numbers=('NWC', 'WIO', 'NWC'), feature_group_count=C)
    return out.reshape(bsz, L, C) + b


def gla_chunked(q, k, v, logf, s0, need_out):
    bsz, L, H, dk = q.shape
    dv = v.shape[-1]
    n = L // A_CHUNK
    rs = lambda t: t.reshape(bsz, n, A_CHUNK, H, t.shape[-1])
    q, k, v = rs(q), rs(k), rs(v)
    b = jnp.cumsum(rs(logf).astype(jnp.float32), axis=2)
    b_last = b[:, :, -1]
    u = jnp.einsum('bnchk,bnchv->bnhkv', k * jnp.exp(b_last[:, :, None] - b), v).astype(jnp.float32)

    def step(s, inp):
        dec, un = inp
        return jnp.exp(dec)[..., None] * s + un, s

    s_fin, s_prev = lax.scan(step, s0.astype(jnp.float32),
                             (jnp.moveaxis(b_last, 1, 0), jnp.moveaxis(u, 1, 0)))
    if not need_out:
        return None, s_fin
    s_prev = jnp.moveaxis(s_prev, 0, 1)
    o_inter = jnp.einsum('bnthk,bnhkv->bnthv', q * jnp.exp(b), s_prev)
    causal = jnp.tril(jnp.ones((A_CHUNK, A_CHUNK), dtype=bool))[None, None, :, :, None, None]
    dec = masked_exp(b[:, :, :, None] - b[:, :, None, :], causal)
    scores = jnp.einsum('bnthk,bntshk,bnshk->bnths', q, dec, k)
    o = o_inter + jnp.einsum('bnths,bnshv->bnthv', scores, v)
    return o.reshape(bsz, L, H, dv), s_fin


def ssd_chunked(xh, dt, a_neg, bm, cm, s0, need_out):
    bsz, L = xh.shape[:2]
    n = L // B_CHUNK
    G, R, N, P = B_GROUPS, B_HPG, B_STATE, B_HEADDIM
    x_dt = (xh * dt[..., None]).reshape(bsz, n, B_CHUNK, G, R, P)
    cum = jnp.cumsum((dt * a_neg).reshape(bsz, n, B_CHUNK, G, R), axis=2)
    cum_last = cum[:, :, -1]
    bc = bm.reshape(bsz, n, B_CHUNK, G, N)
    cc = cm.reshape(bsz, n, B_CHUNK, G, N)
    u = jnp.einsum('bktgn,bktgr,bktgrp->bkgrnp', bc, jnp.exp(cum_last[:, :, None] - cum), x_dt).astype(jnp.float32)

    def step(s, inp):
        dec, un = inp
        return jnp.exp(dec)[..., None, None] * s + un, s

    s_fin, s_prev = lax.scan(step, s0.astype(jnp.float32),
                             (jnp.moveaxis(cum_last, 1, 0), jnp.moveaxis(u, 1, 0)))
    if not need_out:
        return None, s_fin
    s_prev = jnp.moveaxis(s_prev, 0, 1)
    y_inter = jnp.einsum('bktgn,bktgr,bkgrnp->bktgrp', cc, jnp.exp(cum), s_prev)
    cb = jnp.einsum('bktgn,bksgn->bktsg', cc, bc)
    causal = jnp.tril(jnp.ones((B_CHUNK, B_CHUNK), dtype=bool))[None, None, :, :, None, None]
    lmat = masked_exp(cum[:, :, :, None] - cum[:, :, None, :], causal)
    y_intra = jnp.einsum('bktsg,bktsgr,bksgrp->bktgrp', cb, lmat, x_dt)
    return (y_inter + y_intra).reshape(bsz, L, B_HEADS, P), s_fin


def hgrn2_branch(q, fz_f, fz_b, i, g, lb_f, lb_b, norm_w, s0_f, s0_b, need_out):
    bsz, L = q.shape[:2]
    hd = lambda t, d: t.reshape(bsz, L, A_HEADS, d)
    q, i = hd(q, A_DK), hd(i, A_DV)

    def direction(fz, lb, s0, flip):
        lb = lb.reshape(A_HEADS, A_DK)
        f = lb + (1.0 - lb) * jax.nn.sigmoid(hd(fz, A_DK).astype(jnp.float32))
        logf = jnp.log(jnp.maximum(f, F_FLOOR))
        k = 1.0 - f
        qq, kk, vv = q, k, i
        if flip:
            qq, kk, vv, logf = qq[:, ::-1], kk[:, ::-1], vv[:, ::-1], logf[:, ::-1]
        o, s = gla_chunked(qq, kk, vv, logf, s0, need_out)
        if flip and o is not None:
            o = o[:, ::-1]
        return o, s

    o_f, s_f = direction(fz_f, lb_f, s0_f, False)
    o_b, s_b = direction(fz_b, lb_b, s0_b, True)
    if not need_out:
        return None, s_f, s_b
    o = (o_f + o_b).reshape(bsz, L, A_VW)
    y = group_rms(o, A_HEADS) * norm_w.reshape(1, A_DV).repeat(A_HEADS, 0).reshape(A_VW) * jax.nn.silu(g)
    return y, s_f, s_b


def ssd_branch(z, xbc, dtr_f, dtr_b, conv_w, conv_b, dt_bias, a_log, d_skip, norm_w, s0_f, s0_b, n_rows, need_out):
    bsz, L = xbc.shape[:2]
    xbc = jax.nn.silu(dwconv_centred(xbc, conv_w, conv_b, n_rows))
    xs, bm, cm = jnp.split(xbc, (B_WIDTH, B_WIDTH + B_GROUPS * B_STATE), axis=-1)
    xh = xs.reshape(bsz, L, B_HEADS, B_HEADDIM)
    bm = bm.reshape(bsz, L, B_GROUPS, B_STATE)
    cm = cm.reshape(bsz, L, B_GROUPS, B_STATE)

    def direction(dtr, d, s0, flip):
        dt = jax.nn.softplus(dtr.astype(jnp.float32) + dt_bias[d])
        a_neg = -jnp.exp(a_log[d].astype(jnp.float32))
        xx, dd, bb, cc = xh, dt, bm, cm
        if flip:
            xx, dd, bb, cc = xx[:, ::-1], dd[:, ::-1], bb[:, ::-1], cc[:, ::-1]
        y, s = ssd_chunked(xx, dd, a_neg, bb, cc, s0, need_out)
        if flip and y is not None:
            y = y[:, ::-1]
        return y, s

    y_f, s_f = direction(dtr_f, 0, s0_f, False)
    y_b, s_b = direction(dtr_b, 1, s0_b, True)
    if not need_out:
        return None, s_f, s_b
    y = (y_f + y_b + d_skip[:, None] * xh).reshape(bsz, L, B_WIDTH)
    return group_rms(y * jax.nn.silu(z), B_GROUPS) * norm_w, s_f, s_b


def hybrid_mixer(h, w_in_l, lb_f, lb_b, a_norm_w_l, conv_w, conv_b, dt_bias, a_log, d_skip, b_norm_w_l,
                 wpa, wpb, wo, states0, n_rows, need_out):
    u = h @ w_in_l
    aq, af_f, af_b, ai, ag, bz, bxbc, bdt_f, bdt_b, gate_a, gate_b = jnp.split(u, IN_SPLITS, axis=-1)
    ya, sa_f, sa_b = hgrn2_branch(aq, af_f, af_b, ai, ag, lb_f, lb_b, a_norm_w_l,
                                  states0[0], states0[1], need_out)
    yb, sb_f, sb_b = ssd_branch(bz, bxbc, bdt_f, bdt_b, conv_w, conv_b, dt_bias, a_log, d_skip, b_norm_w_l,
                                states0[2], states0[3], n_rows, need_out)
    states = (sa_f, sa_b, sb_f, sb_b)
    if not need_out:
        return None, states
    merged = jax.nn.sigmoid(gate_a) * (ya @ wpa) + jax.nn.sigmoid(gate_b) * (yb @ wpb)
    return merged @ wo, states


def zero_states(bsz):
    sa = jnp.zeros((bsz, A_HEADS, A_DK, A_DV), jnp.float32)
    sb = jnp.zeros((bsz, B_GROUPS, B_HPG, B_STATE, B_HEADDIM), jnp.float32)
    return (sa, sa, sb, sb)


def setup_inputs(seed: int = 0) -> dict:
    key = jax.random.key(seed)
    ks = jax.random.split(key, 20)
    nrm = jax.random.normal
    D = D_MODEL
    x = nrm(ks[0], (BATCH, SEQ, D), jnp.float32)
    c = nrm(ks[1], (BATCH, D), jnp.float32)
    ctx = nrm(ks[2], (BATCH, CTX_LEN, D), jnp.float32)
    c_ctx = nrm(ks[3], (D,), jnp.float32)
    w_mod = nrm(ks[4], (DEPTH, D, 3 * D), jnp.float32) * (0.1 * D ** -0.5)
    gate_one = jnp.concatenate([jnp.zeros((2 * D,), jnp.float32), jnp.ones((D,), jnp.float32)])
    b_mod = 0.02 * nrm(ks[5], (DEPTH, 3 * D), jnp.float32) + gate_one
    w_in = nrm(ks[6], (DEPTH, D, IN_COLS), jnp.float32) * D ** -0.5
    a_lb_logits = 0.5 * nrm(ks[7], (2, DEPTH, A_KW), jnp.float32)
    a_norm_w = 1.0 + 0.02 * nrm(ks[8], (DEPTH, A_DV), jnp.float32)
    b_conv_w = nrm(ks[9], (DEPTH, B_CONV, B_CONV_CH), jnp.float32) * B_CONV ** -0.5
    b_conv_b = 0.02 * nrm(ks[10], (DEPTH, B_CONV_CH), jnp.float32)
    dt0 = jnp.exp(jax.random.uniform(ks[11], (DEPTH, 2, B_HEADS), jnp.float32, np.log(1e-3), np.log(1e-1)))
    b_dt_bias = dt0 + jnp.log(-jnp.expm1(-dt0))
    b_a_log = jnp.log(jax.random.uniform(ks[12], (DEPTH, 2, B_HEADS), jnp.float32, 1.0, 16.0))
    b_d = 1.0 + 0.1 * nrm(ks[13], (DEPTH, B_HEADS), jnp.float32)
    b_norm_w = 1.0 + 0.02 * nrm(ks[14], (DEPTH, B_WIDTH), jnp.float32)
    w_proj_a = nrm(ks[15], (DEPTH, A_VW, D), jnp.float32) * (A_VW ** -0.5 * DN_BETA)
    w_proj_b = nrm(ks[16], (DEPTH, B_WIDTH, D), jnp.float32) * (B_WIDTH ** -0.5 * DN_BETA)
    w_out = nrm(ks[17], (DEPTH, D, D), jnp.float32) * (D ** -0.5 * DN_BETA)
    ln_g = 1.0 + 0.02 * nrm(ks[18], (DEPTH, D), jnp.float32)
    ln_b = 0.02 * nrm(ks[19], (DEPTH, D), jnp.float32)
    return {'x': x, 'c': c, 'ctx': ctx, 'c_ctx': c_ctx, 'w_mod': w_mod, 'b_mod': b_mod, 'w_in': w_in,
            'a_lb_logits': a_lb_logits, 'a_norm_w': a_norm_w, 'b_conv_w': b_conv_w, 'b_conv_b': b_conv_b,
            'b_dt_bias': b_dt_bias, 'b_a_log': b_a_log, 'b_d': b_d, 'b_norm_w': b_norm_w,
            'w_proj_a': w_proj_a, 'w_proj_b': w_proj_b, 'w_out': w_out, 'ln_g': ln_g, 'ln_b': ln_b}


def reference(x, c, ctx, c_ctx, w_mod, b_mod, w_in, a_lb_logits, a_norm_w, b_conv_w, b_conv_b,
              b_dt_bias, b_a_log, b_d, b_norm_w, w_proj_a, w_proj_b, w_out, ln_g, ln_b):
    D = D_MODEL
    bsz, L, _ = x.shape
    rows = L // GRID_W
    sm = jax.nn.softmax(a_lb_logits.astype(jnp.float32), axis=1)
    lower = jnp.cumsum(sm, axis=1) - sm[:, :1]
    c_act = jax.nn.silu(c)
    cc_act = jax.nn.silu(c_ctx)
    xl, xc = x, ctx
    for l in range(DEPTH):
        last = l == DEPTH - 1
        ml = c_act @ w_mod[l] + b_mod[l]
        mc = cc_act @ w_mod[l] + b_mod[l]
        shift_l, scale_l, gate_l = ml[:, None, :D], ml[:, None, D:2 * D], ml[:, None, 2 * D:]
        shift_c, scale_c, gate_c = mc[:D], mc[D:2 * D], mc[2 * D:]
        common = (w_in[l], lower[0, l], lower[1, l], a_norm_w[l], b_conv_w[l], b_conv_b[l], b_dt_bias[l],
                  b_a_log[l], b_d[l], b_norm_w[l], w_proj_a[l], w_proj_b[l], w_out[l])
        hc = xc * (1.0 + scale_c) + shift_c
        out_c, ctx_states = hybrid_mixer(hc, *common, zero_states(bsz), 1, not last)
        hl = xl * (1.0 + scale_l) + shift_l
        out_l, _ = hybrid_mixer(hl, *common, ctx_states, rows, True)
        xl = layer_norm(DN_ALPHA * xl + gate_l * out_l, ln_g[l], ln_b[l])
        if not last:
            xc = layer_norm(DN_ALPHA * xc + gate_c * out_c, ln_g[l], ln_b[l])
    return xl
```

```python
import numpy as np
import ml_dtypes
import concourse.bass as bass
import concourse.mybir as mybir
from concourse.bass_utils import run_bass_kernel_spmd

F32 = mybir.dt.float32
BF16 = mybir.dt.bfloat16
AF = mybir.ActivationFunctionType
ALU = mybir.AluOpType

D = 1024
DEPTH = 4
CTX = 256
SEQ = 2048
T = CTX + SEQ
NT = T // 128
INC = 7712
C_Q, C_FF, C_FB, C_I, C_G, C_Z, C_XS, C_BM, C_CM, C_DT, C_GA, C_GB = 0, 512, 1024, 1536, 2048, 2560, 3584, 4608, 5120, 5632, 5664, 6688
TG = [(0, 256)] + [(256 + 512 * i, 256 + 512 * (i + 1)) for i in range(4)]
DN_ALPHA = (2 * DEPTH) ** 0.25
LN_EPS = 1e-5
RMS_EPS = 1e-6
BIG = 1e17
NEG = -30000.0

PC_LBL = 0
PC_ANW = PC_LBL + 32
PC_CW = PC_ANW + 4
PC_CB = PC_CW + 320
PC_DSK = PC_CB + 64
PC_BNW = PC_DSK + 32
PC_BMOD = PC_BNW + 32
PC_C = PC_BMOD + 96
PC_N = PC_C + 16
K_ID, K_ONE, K_TRIF, K_TRIB, K_MNF, K_MNB, K_HMF, K_HMB, K_RM = [i * 128 for i in range(9)]
K_N = K_RM + 512


class Sched:
    ENG = ['pe', 'act', 'dve', 'pool', 'sp']

    def __init__(self, nc, ndma_sems=8):
        self.nc = nc
        self.q = {e: [] for e in self.ENG}
        self.sem = {e: nc.alloc_semaphore("s_" + e) for e in ('pe', 'act', 'dve', 'pool')}
        self.count = {e: 0 for e in self.ENG}
        self.waited = {e: {} for e in self.ENG}
        self.last_write = {}
        self.readers = {}
        self.dsems = {e: [nc.alloc_semaphore("d_%s%d" % (e, i)) for i in range(ndma_sems)] for e in ('sp', 'pool')}
        self.dcnt = {e: [0] * ndma_sems for e in self.dsems}
        self.drr = {e: 0 for e in self.dsems}
        self.tag = ''
        self.annotate = False

    def _deps(self, eng, reads, writes):
        need = {}

        def add(tok):
            if tok is None:
                return
            key, val = tok
            if need.get(key, 0) < val:
                need[key] = val
        for r in reads:
            add(self.last_write.get(r))
        for w in writes:
            add(self.last_write.get(w))
            for t in self.readers.get(w, ()):
                add(t)
        waits = []
        for key, val in need.items():
            if eng == 'pe' and key == ('c', 'pe'):
                continue
            if self.waited[eng].get(key, 0) < val:
                self.waited[eng][key] = val
                waits.append((key, val))
        return waits

    def _commit(self, tok, reads, writes):
        for r in reads:
            lst = self.readers.setdefault(r, [])
            for i, (k, v) in enumerate(lst):
                if k == tok[0]:
                    lst[i] = tok
                    break
            else:
                lst.append(tok)
        for w in writes:
            self.last_write[w] = tok
            self.readers[w] = []

    def op(self, eng, fn, reads=(), writes=()):
        writes = list(writes) + [r for r in reads if r[0] == 'P']
        reads = [r for r in reads if r[0] != 'P']
        waits = self._deps(eng, reads, writes)
        self.count[eng] += 1
        tok = (('c', eng), self.count[eng])
        self.q[eng].append((waits, fn, (self.sem[eng], 1), self.tag))
        self._commit(tok, reads, writes)

    def dma(self, eng, fn, reads=(), writes=()):
        waits = self._deps(eng, reads, writes)
        k = self.drr[eng]
        self.drr[eng] = (k + 1) % len(self.dsems[eng])
        prev = self.dcnt[eng][k]
        key = ('d', eng, k)
        if prev and self.waited[eng].get(key, 0) < prev:
            self.waited[eng][key] = prev
            waits.append((key, prev))
        self.dcnt[eng][k] = prev + 16
        tok = (key, prev + 16)
        self.q[eng].append((waits, fn, (self.dsems[eng][k], 16), self.tag))
        self._commit(tok, reads, writes)

    def all_tokens(self):
        toks = []
        for e in ('pe', 'act', 'dve', 'pool'):
            if self.count[e]:
                toks.append((('c', e), self.count[e]))
        for e in self.dsems:
            for k, v in enumerate(self.dcnt[e]):
                if v:
                    toks.append((('d', e, k), v))
        return toks

    def barrier(self):
        toks = self.all_tokens()
        for eng in self.ENG:
            waits = []
            for key, val in toks:
                if key == ('c', eng):
                    continue
                if self.waited[eng].get(key, 0) < val:
                    self.waited[eng][key] = val
                    waits.append((key, val))
            if waits:
                self.q[eng].append((waits, None, None, self.tag))

    def _semof(self, key):
        if key[0] == 'c':
            return self.sem[key[1]]
        return self.dsems[key[1]][key[2]]

    def emit(self, block):
        S = self
        self.barrier()

        def run(ename, e):
            for waits, fn, si, tag in S.q[ename]:
                for key, val in waits:
                    e.wait_ge(S._semof(key), val)
                if fn is not None:
                    ins = None
                    for name, a, k in fn:
                        ins = getattr(e, name)(*a, **k)
                        if S.annotate:
                            ins.annotate(tag)
                    ins.then_inc(si[0], si[1])

        @block.tensor
        def _(e):
            run('pe', e)

        @block.scalar
        def _(e):
            run('act', e)

        @block.vector
        def _(e):
            run('dve', e)

        @block.gpsimd
        def _(e):
            run('pool', e)

        @block.sync
        def _(e):
            run('sp', e)


def pipeline(items, stages, skew=1):
    n = len(items)
    for t in range(n + (len(stages) - 1) * skew):
        for s, f in enumerate(stages):
            k = t - s * skew
            if 0 <= k < n:
                f(k, items[k])


def C(name, *a, **k):
    return [(name, a, k)]


def MM(items):
    n = len(items)
    return [('matmul', (o,), dict(lhsT=l, rhs=r, start=(i == 0), stop=(i == n - 1))) for i, (o, l, r) in enumerate(items)]


def build(nlayers=DEPTH, dbg=False, phases=(1, 2, 3, 4), annotate=False):
    nc = bass.Bass("TRN2", target_bir_lowering=False)
    dt_in = lambda n, s, d=F32: nc.dram_tensor(n, s, d, kind="ExternalInput").ap()
    x_in = dt_in("x", [SEQ, D])
    ctx_in = dt_in("ctx", [CTX, D])
    w_mod = dt_in("w_mod", [DEPTH, D, 3 * D])
    w_in = dt_in("w_in", [DEPTH, D, INC])
    w_pa = dt_in("w_proj_a", [DEPTH, 512, D])
    w_pb = dt_in("w_proj_b", [DEPTH, D, D])
    w_o = dt_in("w_out", [DEPTH, D, D])
    ln_g = dt_in("ln_g", [DEPTH, D])
    ln_b = dt_in("ln_b", [DEPTH, D])
    pcol_in = dt_in("pcol_d", [128, PC_N])
    prow_in = dt_in("prow_d", [1, 256])
    cst_in = dt_in("cst_d", [128, K_N])
    out = nc.dram_tensor("out", [SEQ, D], F32, kind="ExternalOutput").ap()
    xres = nc.dram_tensor("xres", [T, D], F32).ap()
    dbg_outs = {}

    S = Sched(nc)
    S.annotate = annotate

    cst = nc.alloc_sbuf_tensor("cst", [128, K_N], F32)
    pcol = nc.alloc_sbuf_tensor("pcol", [128, PC_N], F32)
    prowB = nc.alloc_sbuf_tensor("prowB", [128, 256], F32)
    anegB = nc.alloc_sbuf_tensor("anegB", [128, 128], F32)
    lbs = nc.alloc_sbuf_tensor("lbs", [128, 5, 32], F32)
    identB = nc.alloc_sbuf_tensor("identB", [128, 128], BF16)
    cstB = nc.alloc_sbuf_tensor("cstB", [128, 4, 128], BF16)
    hT = nc.alloc_sbuf_tensor("hT", [128, 8, T], BF16)
    yaT = nc.alloc_sbuf_tensor("yaT", [128, 4, T], BF16)
    ybT = nc.alloc_sbuf_tensor("ybT", [128, 8, T], BF16)
    modT2 = [nc.alloc_sbuf_tensor("modT%d" % i, [128, 24, 2], F32) for i in range(2)]
    cact = nc.alloc_sbuf_tensor("cact", [128, 16], BF16)
    epsR = nc.alloc_sbuf_tensor("epsR", [128, 2], F32)
    NW = 4
    wbuf = [nc.alloc_sbuf_tensor("wbuf%d" % i, [128, 8, 128], BF16) for i in range(NW)]
    wrr = [0]
    nbig = (nc.sbuf_bytes_remaining - 512) // 4
    big = nc.alloc_sbuf_tensor("big", [128, nbig], F32)
    cur = [0]

    def carve(shape, dtype):
        n = int(np.prod(shape))
        words = (n + 1) // 2 if dtype == BF16 else n
        words = (words + 7) // 8 * 8
        a = big[:, cur[0]:cur[0] + words]
        cur[0] += words
        assert cur[0] <= nbig, ("SBUF phase scratch overflow", cur[0], nbig)
        if dtype == BF16:
            a = a.bitcast(BF16)[:, 0:n]
        else:
            a = a[:, 0:n]
        if len(shape) == 2:
            return a.rearrange("p (a b) -> p a b", b=shape[1])
        if len(shape) == 3:
            return a.rearrange("p (a b c) -> p a b c", b=shape[1], c=shape[2])
        return a

    def newphase():
        S.barrier()
        cur[0] = 0

    PA = [nc.alloc_psum_tensor("PA%d" % i, [128, 512], F32) for i in range(2)]
    PB = [nc.alloc_psum_tensor("PB%d" % i, [128, 512], F32) for i in range(2)]
    PC = [nc.alloc_psum_tensor("PC%d" % i, [128, 512], F32) for i in range(2)]
    PD = nc.alloc_psum_tensor("PD", [128, 512], F32)
    PT = nc.alloc_psum_tensor("PT", [128, 1024], BF16)
    rr = {'pa': 0, 'pb': 0, 'pc': 0, 'pd': 0, 'pai': 0, 'pcb': 0}

    def nxt(kind, n):
        v = rr[kind]
        rr[kind] = (v + 1) % n
        return v

    def slot(kind):
        banks = {'pa': PA, 'pb': PB, 'pc': PC}[kind]
        s = nxt(kind, 8)
        b, c = s % 2, (s // 2) * 128
        return '%s%d' % (kind.upper(), b), banks[b][:, c:c + 128]

    identF = cst[:, K_ID:K_ID + 128]
    onesF = cst[:, K_ONE:K_ONE + 128]
    TRI = [cst[:, K_TRIF:K_TRIF + 128], cst[:, K_TRIB:K_TRIB + 128]]
    MNEG = [cst[:, K_MNF:K_MNF + 128], cst[:, K_MNB:K_MNB + 128]]
    HMASK = [cst[:, K_HMF:K_HMF + 128], cst[:, K_HMB:K_HMB + 128]]
    RMASK = cst[:, K_RM:K_RM + 512]

    def pc(off, n=1):
        return pcol[:, off:off + n]

    def load_w(src_ap, kchunks=8, ncols=128):
        i = wrr[0]
        wrr[0] = (i + 1) % NW
        key = 'w%d' % i
        dst = wbuf[i][:, 0:kchunks, 0:ncols]
        S.dma('pool', C('dma_start', out=dst, in_=src_ap.rearrange("(k p) c -> p k c", p=128)), writes=[key])
        return key, wbuf[i]

    def inproj_fm(l, col0, t0, t1, wk=None):
        if wk is None:
            wk = load_w(w_in[l, :, col0:col0 + 128])
        wkey, wt = wk
        i = nxt('pai', 2)
        pskey = 'PA%d' % i
        ps = PA[i][:, 0:t1 - t0]
        S.op('pe', MM([(ps, wt[:, k, :], hT[:, k, t0:t1]) for k in range(8)]), reads=[wkey, 'hT'], writes=[pskey])
        return pskey, ps

    def dump(name, ap, reads):
        if not dbg:
            return
        shp = list(ap.shape)
        d = nc.dram_tensor("dbg_" + name, shp, ap.dtype, kind="ExternalOutput").ap()
        dbg_outs[name] = shp
        S.dma('sp', C('dma_start', out=d, in_=ap), reads=reads)

    S.dma('sp', C('dma_start', out=cst[:], in_=cst_in), writes=['cst'])
    S.dma('sp', C('dma_start', out=pcol[:], in_=pcol_in), writes=['pcol'])
    S.dma('sp', C('dma_start', out=prowB[:], in_=prow_in.partition_broadcast(128)), writes=['prowB'])
    S.op('dve', C('tensor_copy', out=identB[:], in_=identF), reads=['cst'], writes=['identB'])
    S.op('dve', C('tensor_copy', out=cstB[:], in_=cst[:, K_TRIF:K_TRIF + 512].rearrange("p (a b) -> p a b", b=128)), reads=['cst'], writes=['cstB'])
    TRIB = [cstB[:, 0, :], cstB[:, 1, :]]
    MNEGB = [cstB[:, 2, :], cstB[:, 3, :]]
    S.op('pool', C('memset', epsR[:, 0:1], RMS_EPS), writes=['epsR'])
    S.op('pool', C('memset', epsR[:, 1:2], LN_EPS), reads=[], writes=['epsR'])
    S.op('act', C('activation', out=anegB[:], in_=prowB[:, 128:256], func=AF.Exp), reads=['prowB'], writes=['anegB'])
    S.op('act', C('mul', out=anegB[:], in_=anegB[:], mul=-1.0), reads=['anegB'], writes=['anegB'])
    S.op('act', C('activation', out=cact[:], in_=pc(PC_C, 16), func=AF.Silu), reads=['pcol'], writes=['cact'])
    lb4 = lambda j: lbs[:, j, :].rearrange("p (d l a) -> p d l a", d=2, l=4)
    S.op('act', C('activation', out=lbs[:, 0, :], in_=pc(PC_LBL, 32), func=AF.Exp), reads=['pcol'], writes=['lbs'])
    E4 = lb4(0)
    tmp4 = lb4(4)
    S.op('dve', C('tensor_tensor', out=tmp4[:, :, 0, :], in0=E4[:, :, 0, :], in1=E4[:, :, 1, :], op=ALU.add), reads=['lbs'], writes=['lbs'])
    S.op('dve', C('tensor_tensor', out=tmp4[:, :, 0, :], in0=tmp4[:, :, 0, :], in1=E4[:, :, 2, :], op=ALU.add), reads=['lbs'], writes=['lbs'])
    S.op('dve', C('tensor_tensor', out=tmp4[:, :, 0, :], in0=tmp4[:, :, 0, :], in1=E4[:, :, 3, :], op=ALU.add), reads=['lbs'], writes=['lbs'])
    S.op('dve', C('reciprocal', out=tmp4[:, :, 1, :], in_=tmp4[:, :, 0, :]), reads=['lbs'], writes=['lbs'])
    for l in range(1, 4):
        S.op('dve', C('tensor_tensor', out=E4[:, :, l, :], in0=E4[:, :, l, :], in1=tmp4[:, :, 1, :], op=ALU.mult), reads=['lbs'], writes=['lbs'])
    LO = lb4(1)
    S.op('dve', C('memset', LO[:, :, 0, :], 0.0), reads=['lbs'], writes=['lbs'])
    S.op('dve', C('tensor_copy', out=LO[:, :, 1, :], in_=E4[:, :, 1, :]), reads=['lbs'], writes=['lbs'])
    S.op('dve', C('tensor_tensor', out=LO[:, :, 2, :], in0=LO[:, :, 1, :], in1=E4[:, :, 2, :], op=ALU.add), reads=['lbs'], writes=['lbs'])
    S.op('dve', C('tensor_tensor', out=LO[:, :, 3, :], in0=LO[:, :, 2, :], in1=E4[:, :, 3, :], op=ALU.add), reads=['lbs'], writes=['lbs'])
    S.op('dve', C('tensor_scalar', out=lbs[:, 2, :], in0=lbs[:, 1, :], scalar1=-1.0, scalar2=1.0, op0=ALU.mult, op1=ALU.add), reads=['lbs'], writes=['lbs'])
    S.op('dve', C('tensor_scalar', out=lbs[:, 3, :], in0=lbs[:, 1, :], scalar1=-1.0, scalar2=None, op0=ALU.add), reads=['lbs'], writes=['lbs'])

    def lbcol(j, d, l, a):
        o = d * 16 + l * 4 + a
        return lbs[:, j, o:o + 1]

    def emit_mod(ll):
        S.tag = 'mod'
        mT = modT2[ll % 2]
        mk = 'modT%d' % (ll % 2)
        for nb in range(6):
            wks = []
            for k2 in range(2):
                pass
            for cc in range(4):
                ch = nb * 4 + cc
                wkey, wt = load_w(w_mod[ll, :, ch * 128:(ch + 1) * 128])
                ps = PD[:, ch * 2:ch * 2 + 2]
                S.op('pe', MM([(ps, wt[:, k, :], cact[:, k:k + 9:8]) for k in range(8)]),
                     reads=[wkey, 'cact'], writes=['PD'])
        S.op('dve', C('tensor_tensor', out=mT[:], in0=PD[:, 0:48].rearrange("p (c w) -> p c w", w=2),
                                              in1=pc(PC_BMOD + ll * 24, 24).unsqueeze(2).broadcast_to([128, 24, 2]), op=ALU.add),
             reads=['PD', 'pcol'], writes=[mk])
        S.op('dve', C('tensor_scalar', out=mT[:, 8:16, :], in0=mT[:, 8:16, :], scalar1=1.0, scalar2=None, op0=ALU.add), reads=[mk], writes=[mk])


    for l in range(nlayers):
        last = (l == DEPTH - 1)
        newphase()
        if l == 0:
            emit_mod(0)
        modT = modT2[l % 2]
        MK = 'modT%d' % (l % 2)
        def bcast_tile(dst, dkey, part, which):
            for half in range(2):
                for jj in range(4):
                    j = half * 4 + jj
                    S.op('pe', C('matmul', PD[:, jj * 128:(jj + 1) * 128], lhsT=modT[:, part * 8 + j, which:which + 1].broadcast_to([128, 128]),
                                                              rhs=identF, start=True, stop=True), reads=[MK, 'cst'], writes=['PD'])
                S.op('act', C('copy', out=dst[:, half * 512:(half + 1) * 512], in_=PD[:, :]), reads=['PD'], writes=[dkey])

        shB = [carve([D], F32), carve([D], F32)]
        scB = [carve([D], F32), carve([D], F32)]
        for w in range(2):
            bcast_tile(shB[w], 'shB%d' % w, 0, w)
            bcast_tile(scB[w], 'scB%d' % w, 1, w)
        S.tag = 'p1'
        xt = [carve([D], F32), carve([D], F32)]
        h1 = carve([D], F32)
        hb = [carve([D], BF16), carve([D], BF16)]
        for i in range(NT):
            w = 1 if i < 2 else 0
            b = i % 2
            if l == 0:
                src = ctx_in[i * 128:(i + 1) * 128, :] if i < 2 else x_in[(i - 2) * 128:(i - 1) * 128, :]
            else:
                src = xres[i * 128:(i + 1) * 128, :]
            S.dma('sp', C('dma_start', out=xt[b], in_=src), reads=['xres%d' % i], writes=['xt%d' % b])
            S.op('dve', C('tensor_tensor', out=h1, in0=xt[b], in1=scB[w], op=ALU.mult), reads=['xt%d' % b, 'scB%d' % w], writes=['h1'])
            S.op('pool', C('tensor_tensor', out=hb[b], in0=h1, in1=shB[w], op=ALU.add), reads=['h1', 'shB%d' % w], writes=['hb%d' % b])
            S.op('pe', sum([C('transpose', out=PT[:, k * 128:(k + 1) * 128], in_=hb[b][:, k * 128:(k + 1) * 128], identity=identB[:]) for k in range(8)], []),
                 reads=['hb%d' % b, 'identB'], writes=['PT'])
            S.op('act', C('copy', out=hT[:, :, i * 128:(i + 1) * 128], in_=PT[:, :].rearrange("p (k t) -> p k t", t=128)), reads=['PT'], writes=['hT'])

        newphase()
        HEADS = (range(4) if 2 in phases else []) if 22 not in phases else [0]
        qT2 = [carve([T], BF16), carve([T], BF16)]
        vtok2 = [carve([NT, 128], BF16), carve([NT, 128], BF16)]
        qe = carve([T], BF16)
        ke = carve([T], BF16)
        keT = carve([NT, 128], BF16)
        oacc = carve([T], F32)
        dec = carve([36], F32)
        em = carve([36], F32)
        etm = carve([36], F32)
        Sst2 = [carve([128], F32) for _ in range(4)]
        Ue = [carve([128], F32) for _ in range(4)]
        Sball = carve([36, 128], BF16)
        scm = [carve([128], BF16) for _ in range(4)]
        sg = [carve([512], F32) for _ in range(2)]
        lf = [carve([512], F32) for _ in range(2)]
        kk = [carve([512], F32) for _ in range(3)]
        cum = [carve([512], F32) for _ in range(2)]
        bb = [carve([512], F32) for _ in range(2)]
        dd = [carve([512], F32) for _ in range(2)]
        ee = [carve([512], F32) for _ in range(4)]
        gs = [carve([512], F32), carve([512], F32)]
        r1 = [carve([512], F32), carve([512], F32)]
        def emit_qv(a):
            qT, vtok = qT2[a % 2], vtok2[a % 2]
            QK, VK = 'qT%d' % (a % 2), 'vtok%d' % (a % 2)
            S.tag = 'p2.qv.a%d' % a
            wk = load_w(w_in[l, :, C_Q + a * 128:C_Q + (a + 1) * 128])
            for (t0, t1) in TG:
                pk, ps = inproj_fm(l, None, t0, t1, wk)
                S.op('act', C('copy', out=qT[:, t0:t1], in_=ps), reads=[pk], writes=[QK])
            wkey, wt = load_w(w_in[l, :, C_I + a * 128:C_I + (a + 1) * 128])
            for i in range(NT):
                sv, ps = slot('pb')
                S.op('pe', MM([(ps, hT[:, k, i * 128:(i + 1) * 128], wt[:, k, :]) for k in range(8)]),
                     reads=[wkey, 'hT'], writes=[sv])
                S.op('act', C('copy', out=vtok[:, i, :], in_=ps), reads=[sv], writes=[VK])
        HL = list(HEADS)
        if HL:
            emit_qv(HL[0])
        for hi, a in enumerate(HL):
            qT, vtok = qT2[a % 2], vtok2[a % 2]
            QK, VK = 'qT%d' % (a % 2), 'vtok%d' % (a % 2)
            for d in range(2):
                wk = load_w(w_in[l, :, (C_FF, C_FB)[d] + a * 128:(C_FF, C_FB)[d] + (a + 1) * 128])
                LBc, OMLc, NOMLc = lbcol(1, d, l, a), lbcol(2, d, l, a), lbcol(3, d, l, a)
                S.tag = 'p2.prep.a%dd%d' % (a, d)

                def v3(ap, n):
                    return ap[:, 0:n].rearrange("p (c t) -> p c t", t=64)

                def p0(k, tg):
                    t0, t1 = tg
                    n = t1 - t0
                    pk, ps = inproj_fm(l, None, t0, t1, wk)
                    S.op('act', C('activation', out=sg[k % 2][:, 0:n], in_=ps, func=AF.Exp, scale=-1.0), reads=[pk], writes=['sg%d' % (k % 2)])
                    S.op('act', C('activation', out=lf[k % 2][:, 0:n], in_=sg[k % 2][:, 0:n], func=AF.Ln, bias=1.0), reads=['sg%d' % (k % 2)], writes=['lf%d' % (k % 2)])
                    S.op('act', C('activation', out=sg[k % 2][:, 0:n], in_=lf[k % 2][:, 0:n], func=AF.Exp, scale=-1.0), reads=['lf%d' % (k % 2)], writes=['sg%d' % (k % 2)])
                    S.op('act', C('activation', out=lf[k % 2][:, 0:n], in_=sg[k % 2][:, 0:n], func=AF.Ln, scale=OMLc, bias=LBc), reads=['sg%d' % (k % 2), 'lbs'], writes=['lf%d' % (k % 2)])
                    S.op('pool', C('tensor_scalar', out=kk[k % 3][:, 0:n], in0=sg[k % 2][:, 0:n], scalar1=NOMLc, scalar2=OMLc, op0=ALU.mult, op1=ALU.add),
                         reads=['sg%d' % (k % 2), 'lbs'], writes=['kk%d' % (k % 3)])

                def p1(k, tg):
                    t0, t1 = tg
                    n = t1 - t0
                    ncn, c0 = n // 64, t0 // 64
                    cu, lfk = cum[k % 2], lf[k % 2]
                    ckey = 'cum%d' % (k % 2)
                    S.op('dve', C('tensor_tensor_scan', out=cu[:, 0:n], data0=RMASK[:, 0:n], data1=lfk[:, 0:n], initial=0.0, op0=ALU.mult, op1=ALU.add),
                         reads=['lf%d' % (k % 2), 'cst'], writes=[ckey])
                    tot1 = v3(cu, n)[:, :, 63:64]
                    if d == 0:
                        bsrc, bkey = cu, ckey
                    else:
                        bsrc, bkey = bb[k % 2], 'bb%d' % (k % 2)
                        S.op('pool', C('tensor_tensor', out=bsrc[:, 0:n], in0=cu[:, 0:n], in1=lfk[:, 0:n], op=ALU.subtract), reads=[ckey, 'lf%d' % (k % 2)], writes=[bkey])
                        S.op('dve', C('tensor_tensor', out=v3(bsrc, n), in0=tot1.broadcast_to([128, ncn, 64]), in1=v3(bsrc, n), op=ALU.subtract), reads=[ckey, bkey], writes=[bkey])
                    mid1 = v3(bsrc, n)[:, :, 32:33]
                    S.op('pool', C('tensor_tensor', out=v3(dd[k % 2], n), in0=v3(bsrc, n), in1=mid1.broadcast_to([128, ncn, 64]), op=ALU.subtract), reads=[bkey], writes=['dd%d' % (k % 2)])
                    S.op('act', C('activation', out=dec[:, c0:c0 + ncn].unsqueeze(2), in_=tot1, func=AF.Exp), reads=[ckey], writes=['dec'])
                    S.op('act', C('activation', out=em[:, c0:c0 + ncn].unsqueeze(2), in_=mid1, func=AF.Exp), reads=[bkey], writes=['em'])
                    S.op('dve', C('tensor_tensor', out=etm[:, c0:c0 + ncn].unsqueeze(2), in0=tot1, in1=mid1, op=ALU.subtract), reads=[ckey, bkey], writes=['etm'])
                    S.op('act', C('activation', out=etm[:, c0:c0 + ncn], in_=etm[:, c0:c0 + ncn], func=AF.Exp), reads=['etm'], writes=['etm'])

                def p2(k, tg):
                    t0, t1 = tg
                    n = t1 - t0
                    e1, e2 = ee[(2 * k) % 4], ee[(2 * k + 1) % 4]
                    k1, k2 = 'ee%d' % ((2 * k) % 4), 'ee%d' % ((2 * k + 1) % 4)
                    S.op('act', C('activation', out=e1[:, 0:n], in_=dd[k % 2][:, 0:n], func=AF.Exp), reads=['dd%d' % (k % 2)], writes=[k1])
                    S.op('act', C('activation', out=e2[:, 0:n], in_=dd[k % 2][:, 0:n], func=AF.Exp, scale=-1.0), reads=['dd%d' % (k % 2)], writes=[k2])
                    S.op('dve', C('scalar_tensor_tensor', out=qe[:, t0:t1], in0=e1[:, 0:n], scalar=BIG, in1=qT[:, t0:t1], op0=ALU.min, op1=ALU.mult),
                         reads=[k1, QK], writes=['qe'])
                    S.op('dve', C('scalar_tensor_tensor', out=ke[:, t0:t1], in0=e2[:, 0:n], scalar=BIG, in1=kk[k % 3][:, 0:n], op0=ALU.min, op1=ALU.mult),
                         reads=[k2, 'kk%d' % (k % 3)], writes=['ke'])
                    nb = n // 128
                    S.op('pe', sum([C('transpose', out=PT[:, j * 128:(j + 1) * 128], in_=ke[:, t0 + j * 128:t0 + (j + 1) * 128], identity=identB[:]) for j in range(nb)], []),
                         reads=['ke', 'identB'], writes=['PT'])
                    S.op('act', C('copy', out=keT[:, t0 // 128:t0 // 128 + nb, :], in_=PT[:, 0:nb * 128].rearrange("p (j k) -> p j k", k=128)),
                         reads=['PT'], writes=['keT'])
                pipeline(TG, [p0, p1, p2])
                if d == 0 and hi + 1 < len(HL):
                    emit_qv(HL[hi + 1])
                S.tag = 'p2.pass1.a%dd%d' % (a, d)
                order = list(range(NT)) if d == 0 else [1, 0] + list(range(NT - 1, 1, -1))
                corder = []
                for i in order:
                    for c in ((0, 1) if d == 0 else (1, 0)):
                        corder.append(i * 2 + c)
                S.op('dve', C('memset', Sst2[0], 0.0), writes=['Sst0'])
                S.op('pool', C('memset', Sball[:, corder[0], :], 0.0), writes=['Sb%d' % corder[0]])
                Ue4 = Ue

                def u0(q, ci):
                    i, c = ci // 2, ci % 2
                    pr = slice(c * 64, (c + 1) * 64)
                    s3, psu = slot('pa')
                    S.op('pe', C('matmul', psu, lhsT=keT[pr, i, :], rhs=vtok[pr, i, :], start=True, stop=True), reads=['keT', VK], writes=[s3])
                    S.op('act', C('activation', out=Ue4[q % 4], in_=psu, func=AF.Identity, scale=etm[:, ci:ci + 1]), reads=[s3, 'etm'], writes=['Ue%d' % (q % 4)])

                def u1(q, ci):
                    cs_, ns_ = q % 4, (q + 1) % 4
                    S.op('dve', C('scalar_tensor_tensor', out=Sst2[ns_], in0=Sst2[cs_], scalar=dec[:, ci:ci + 1], in1=Ue4[q % 4], op0=ALU.mult, op1=ALU.add),
                         reads=['Sst%d' % cs_, 'dec', 'Ue%d' % (q % 4)], writes=['Sst%d' % ns_])
                    nx = corder[q + 1]
                    S.op('act', C('activation', out=Sball[:, nx, :], in_=Sst2[ns_], func=AF.Identity, scale=em[:, nx:nx + 1]), reads=['Sst%d' % ns_, 'em'], writes=['Sb%d' % nx])
                S.tag = 'p2.pass1.a%dd%d' % (a, d)
                pipeline(corder[:-1], [u0, u1], skew=2)
                S.tag = 'p2.pass2.a%dd%d' % (a, d)
                st1 = {}

                def stage1(i):
                    tsl = slice(i * 128, (i + 1) * 128)
                    s1, pss = slot('pb')
                    S.op('pe', C('matmul', pss, lhsT=ke[:, tsl], rhs=qe[:, tsl], start=True, stop=True), reads=['ke', 'qe'], writes=[s1])
                    sm = i % 4
                    S.op('dve', C('tensor_tensor', out=scm[sm], in0=pss, in1=HMASK[d], op=ALU.mult), reads=[s1, 'cst'], writes=['scm%d' % sm])

                def stage2(i):
                    tsl = slice(i * 128, (i + 1) * 128)
                    sm = i % 4
                    s2, pso = slot('pc')
                    items = [C('matmul', pso, lhsT=vtok[:, i, :], rhs=scm[sm], start=True, stop=False)]
                    for c in (0, 1):
                        tk = slice(i * 128 + c * 64, i * 128 + (c + 1) * 64)
                        items.append(C('matmul', pso[:, c * 64:(c + 1) * 64], lhsT=Sball[:, i * 2 + c, :], rhs=qe[:, tk], start=False, stop=(c == 1)))
                    S.op('pe', sum(items, []), reads=[VK, 'scm%d' % sm, 'Sb%d' % (i * 2), 'Sb%d' % (i * 2 + 1), 'qe'], writes=[s2])
                    if d == 0:
                        S.op('act', C('copy', out=oacc[:, tsl], in_=pso), reads=[s2], writes=['oacc%d' % i])
                    else:
                        S.op('dve', C('tensor_tensor', out=oacc[:, tsl], in0=oacc[:, tsl], in1=pso, op=ALU.add), reads=[s2, 'oacc%d' % i], writes=['oacc%d' % i])
                pipeline(list(range(NT)), [lambda k, i: stage1(i), lambda k, i: stage2(i)], skew=3)
            S.tag = 'p2.fin.a%d' % a
            wk = load_w(w_in[l, :, C_G + a * 128:C_G + (a + 1) * 128])

            def f0(k, tg):
                t0, t1 = tg
                n = t1 - t0
                b = k % 2
                okeys = ['oacc%d' % ti for ti in range(t0 // 128, t1 // 128)]
                pk, ps = inproj_fm(l, None, t0, t1, wk)
                S.op('act', C('activation', out=gs[b][:, 0:n], in_=ps, func=AF.Exp, scale=-1.0), reads=[pk], writes=['gs%d' % b])
                S.op('act', C('activation', out=gs[b][:, 0:n], in_=gs[b][:, 0:n], func=AF.Ln, bias=1.0), reads=['gs%d' % b], writes=['gs%d' % b])
                S.op('act', C('activation', out=gs[b][:, 0:n], in_=gs[b][:, 0:n], func=AF.Exp, scale=-1.0), reads=['gs%d' % b], writes=['gs%d' % b])
                S.op('dve', C('tensor_tensor', out=gs[b][:, 0:n], in0=ps, in1=gs[b][:, 0:n], op=ALU.mult), reads=[pk, 'gs%d' % b], writes=['gs%d' % b])
                S.op('act', C('activation', out=r1[b][:, 0:n], in_=oacc[:, t0:t1], func=AF.Square), reads=okeys, writes=['r1%d' % b])
                S.op('pe', C('matmul', PC[b][:, 0:n], lhsT=onesF, rhs=r1[b][:, 0:n], start=True, stop=True), reads=['r1%d' % b, 'cst'], writes=['PC%d' % b])

            def f1(k, tg):
                t0, t1 = tg
                n = t1 - t0
                b = k % 2
                okeys = ['oacc%d' % ti for ti in range(t0 // 128, t1 // 128)]
                S.op('act', C('activation', out=r1[b][:, 0:n], in_=PC[b][:, 0:n], func=AF.Ln, scale=1.0 / 128, bias=epsR[:, 0:1]), reads=['PC%d' % b, 'epsR'], writes=['r1%d' % b])
                S.op('act', C('activation', out=r1[b][:, 0:n], in_=r1[b][:, 0:n], func=AF.Exp, scale=-0.5), reads=['r1%d' % b], writes=['r1%d' % b])
                S.op('dve', C('scalar_tensor_tensor', out=r1[b][:, 0:n], in0=oacc[:, t0:t1], scalar=pc(PC_ANW + l), in1=r1[b][:, 0:n], op0=ALU.mult, op1=ALU.mult),
                     reads=okeys + ['r1%d' % b, 'pcol'], writes=['r1%d' % b])
                S.op('dve', C('tensor_tensor', out=yaT[:, a, t0:t1], in0=r1[b][:, 0:n], in1=gs[b][:, 0:n], op=ALU.mult), reads=['r1%d' % b, 'gs%d' % b], writes=['yaT'])
            pipeline(TG, [f0, f1])
        if l == 0:
            dump("yaT", yaT[:], ['yaT'])

        newphase()
        dtT = carve([NT, 32], F32)
        dahi = carve([NT, 32], BF16)
        dalo = carve([NT, 32], BF16)
        acc_off = cur[0]
        acc2 = [carve([512], F32), carve([512], F32)]
        cumT = big[:, acc_off:acc_off + NT * 32].rearrange("p (a b) -> p a b", b=32)
        ncumT = carve([NT, 32], F32)
        decS = carve([NT, 32], F32)
        edT = carve([NT, 32], F32)
        xsT = carve([2, T], BF16)
        bmT = carve([T], BF16)
        cmT = carve([T], BF16)
        xstok = carve([NT, 256], BF16)
        bctok = carve([NT, 128], BF16)
        yacc = carve([2, T], F32)
        Ss2 = [carve([256], F32) for _ in range(3)]
        Ssball = carve([NT, 256], BF16)
        xdtall = carve([NT, 256], BF16)
        xdd = [carve([256], BF16), carve([256], BF16)]
        Lt = [carve([128], F32) for _ in range(4)]
        sq = big[:, cur[0] - 512:cur[0]]
        SQK = ['Lt0', 'Lt1', 'Lt2', 'Lt3']
        Et = [carve([128], F32) for _ in range(4)]
        sqb = big[:, cur[0] - 512:cur[0]]
        SQ2 = [(sq, SQK), (sqb, ['Et0', 'Et1', 'Et2', 'Et3'])]
        Wt = [carve([128], BF16) for _ in range(4)]
        Vt = [carve([128], BF16) for _ in range(4)]
        y2_off = cur[0]
        y2 = [[carve([256], F32), carve([256], F32)] for _ in range(2)]
        daT = big[:, y2_off:y2_off + NT * 32].rearrange("p (a b) -> p a b", b=32)
        S.tag = 'p3.pro'
        wkey, wt = load_w(w_in[l, :, C_DT:C_DT + 32], ncols=32)
        for i in range(NT):
            S.op('pe', MM([(PD[:, 0:32], hT[:, k, i * 128:(i + 1) * 128], wt[:, k, 0:32]) for k in range(8)]), reads=[wkey, 'hT'], writes=['PD'])
            S.op('dve', C('tensor_tensor', out=dtT[:, i, :], in0=PD[:, 0:32], in1=prowB[:, l * 32:(l + 1) * 32], op=ALU.add), reads=['PD', 'prowB'], writes=['dtT'])
        S.op('act', C('activation', out=dtT[:], in_=dtT[:], func=AF.Exp), reads=['dtT'], writes=['dtT'])
        S.op('act', C('activation', out=dtT[:], in_=dtT[:], func=AF.Ln, bias=1.0), reads=['dtT'], writes=['dtT'])
        S.op('dve', C('tensor_tensor', out=daT[:], in0=dtT[:], in1=anegB[:, l * 32:(l + 1) * 32].unsqueeze(1).broadcast_to([128, NT, 32]), op=ALU.mult),
             reads=['dtT', 'anegB'], writes=['daT'])
        S.op('act', C('copy', out=dahi[:], in_=daT[:]), reads=['daT'], writes=['dahi'])
        S.op('dve', C('tensor_tensor', out=dalo[:], in0=daT[:], in1=dahi[:], op=ALU.subtract), reads=['daT', 'dahi'], writes=['dalo'])
        for i in range(NT):
            S.op('pe', C('matmul', PD[:, 128:144], lhsT=TRI[0], rhs=daT[:, i, 0:16], start=True, stop=True)
                 + C('matmul', PD[:, 144:160], lhsT=TRI[1], rhs=daT[:, i, 16:32], start=True, stop=True)
                 + C('matmul', PD[:, 160:192], lhsT=onesF, rhs=daT[:, i, :], start=True, stop=True),
                 reads=['daT', 'cst'], writes=['PD'])
            S.op('act', C('copy', out=cumT[:, i, :], in_=PD[:, 128:160]), reads=['PD'], writes=['cumT'])
            S.op('act', C('activation', out=decS[:, i, :], in_=PD[:, 160:192], func=AF.Exp), reads=['PD'], writes=['decS'])
            S.op('dve', C('tensor_tensor', out=edT[:, i, :], in0=PD[:, 160:192], in1=cumT[:, i, :], op=ALU.subtract), reads=['PD', 'cumT'], writes=['edT'])
        S.op('act', C('activation', out=edT[:], in_=edT[:], func=AF.Exp), reads=['edT'], writes=['edT'])
        S.op('act', C('mul', out=ncumT[:], in_=cumT[:], mul=-1.0), reads=['cumT'], writes=['ncumT'])

        S.barrier()
        P3 = max([p for p in phases if 30 <= p < 40] + [39 if 3 in phases else 0])
        for g in (range(4) if 3 in phases else ([0] if P3 >= 32 else [])):
            S.tag = 'p3.conv.g%d' % g
            specs = [(C_XS + g * 256, 8 * g // 4 * 0 + (g * 2), xsT[:, 0, :]), (C_XS + g * 256 + 128, g * 2 + 1, xsT[:, 1, :]),
                     (C_BM + g * 128, 8 + g, bmT), (C_CM + g * 128, 12 + g, cmT)]
            citems = []
            for (col0, cidx, dst) in specs:
                wk = load_w(w_in[l, :, col0:col0 + 128])
                for tg in TG:
                    citems.append((wk, cidx, dst, tg))
            cst_ = {}

            def c0(k, it):
                wk, cidx, dst, (t0, t1) = it
                n = t1 - t0
                pk, ps = inproj_fm(l, None, t0, t1, wk)
                cst_[k] = (pk, ps)
                S.op('act', C('activation', out=acc2[k % 2][:, 0:n], in_=ps, func=AF.Identity, scale=pc(PC_CW + (l * 16 + cidx) * 5 + 2), bias=pc(PC_CB + l * 16 + cidx)),
                     reads=[pk, 'pcol'], writes=['acc%d' % (k % 2)])

            def c1(k, it):
                wk, cidx, dst, (t0, t1) = it
                n = t1 - t0
                RW = 256 if t0 == 0 else 64
                pk, ps = cst_.pop(k)
                u3 = ps.rearrange("p (r w) -> p r w", w=RW)
                a3 = acc2[k % 2][:, 0:n].rearrange("p (r w) -> p r w", w=RW)
                for j in (0, 1, 3, 4):
                    sh = j - 2
                    lo, hi = max(0, -sh), RW - max(0, sh)
                    S.op('dve', C('scalar_tensor_tensor', out=a3[:, :, lo:hi], in0=u3[:, :, lo + sh:hi + sh], scalar=pc(PC_CW + (l * 16 + cidx) * 5 + j), in1=a3[:, :, lo:hi],
                                  op0=ALU.mult, op1=ALU.add), reads=[pk, 'acc%d' % (k % 2), 'pcol'], writes=['acc%d' % (k % 2)])
                S.op('act', C('activation', out=dst[:, t0:t1], in_=acc2[k % 2][:, 0:n], func=AF.Silu), reads=['acc%d' % (k % 2)], writes=['xbc'])
            pipeline(citems, [c0, c1])
            S.tag = 'p3.tok.g%d' % g
            for i in (range(NT) if P3 >= 33 else []):
                tsl = slice(i * 128, (i + 1) * 128)
                S.op('pe', C('transpose', out=PT[:, 0:128], in_=xsT[:, 0, tsl], identity=identB[:])
                     + C('transpose', out=PT[:, 128:256], in_=xsT[:, 1, tsl], identity=identB[:])
                     + C('transpose', out=PT[:, 256:384], in_=bmT[:, tsl], identity=identB[:]), reads=['xbc', 'identB'], writes=['PT'])
                S.op('act', C('copy', out=xstok[:, i, :], in_=PT[:, 0:256]), reads=['PT'], writes=['xstok'])
                S.op('dve', C('tensor_copy', out=bctok[:, i, :], in_=PT[:, 256:384]), reads=['PT'], writes=['bctok'])
            for d in (range(2) if P3 >= 34 else []):
                hd0 = d * 16 + g * 4
                order = list(range(NT)) if d == 0 else [1, 0] + list(range(NT - 1, 1, -1))
                S.op('dve', C('memset', Ss2[0], 0.0), writes=['Ss0'])
                S.op('pool', C('memset', Ssball[:, order[0], :], 0.0), writes=['Ssb%d' % order[0]])

                def chain_a(q):
                    i = order[q]
                    S.op('dve', C('tensor_tensor', out=xdtall[:, i, :].rearrange("p (h q) -> p h q", q=64), in0=xstok[:, i, :].rearrange("p (h q) -> p h q", q=64),
                                  in1=dtT[:, i, hd0:hd0 + 4].unsqueeze(2).broadcast_to([128, 4, 64]), op=ALU.mult),
                         reads=['xstok', 'dtT'], writes=['xdt%d' % i])
                    if q + 1 == len(order):
                        return
                    xb_ = q % 2
                    S.op('dve', C('tensor_tensor', out=xdd[xb_][:, :].rearrange("p (h q) -> p h q", q=64), in0=xdtall[:, i, :].rearrange("p (h q) -> p h q", q=64),
                                  in1=edT[:, i, hd0:hd0 + 4].unsqueeze(2).broadcast_to([128, 4, 64]), op=ALU.mult),
                         reads=['xdt%d' % i, 'edT'], writes=['xdd%d' % xb_])

                def chain_b(q):
                    i = order[q]
                    if q + 1 == len(order):
                        return
                    xb_ = q % 2
                    ukey, ups = 'PA%d' % (q % 2), PA[q % 2][:, 256:512]
                    S.op('pe', C('matmul', ups, lhsT=bctok[:, i, :], rhs=xdd[xb_][:, :], start=True, stop=True), reads=['bctok', 'xdd%d' % xb_], writes=[ukey])

                def chain_c(q):
                    i = order[q]
                    if q + 1 == len(order):
                        return
                    ukey, ups = 'PA%d' % (q % 2), PA[q % 2][:, 256:512]
                    cs_, ns_ = q % 3, (q + 1) % 3
                    S.op('dve', C('tensor_tensor', out=Ss2[ns_][:, :].rearrange("p (h q) -> p h q", q=64), in0=Ss2[cs_][:, :].rearrange("p (h q) -> p h q", q=64),
                                  in1=decS[:, i, hd0:hd0 + 4].unsqueeze(2).broadcast_to([128, 4, 64]), op=ALU.mult),
                         reads=['Ss%d' % cs_, 'decS'], writes=['Ss%d' % ns_])
                    S.op('dve', C('tensor_tensor', out=Ss2[ns_], in0=Ss2[ns_], in1=ups, op=ALU.add), reads=['Ss%d' % ns_, ukey], writes=['Ss%d' % ns_])
                    nx = order[q + 1]
                    S.op('act', C('copy', out=Ssball[:, nx, :], in_=Ss2[ns_]), reads=['Ss%d' % ns_], writes=['Ssb%d' % nx])

                def chain_step(q):
                    chain_a(q)
                    chain_b(q)
                    chain_c(q)
                chain_step(0)
                S.tag = 'p3.pass2.g%dd%d' % (g, d)
                S.tag = 'p3.pass2.g%dd%d' % (g, d)
                jobs = [(i, r) for i in order for r in range(4)]
                qpos = {i: q for q, i in enumerate(order)}
                cbs = {}
                ysl = {}

                def stage1(k, i, r):
                    tsl = slice(i * 128, (i + 1) * 128)
                    hd = hd0 + r
                    if qpos[i] + 1 < len(order):
                        if r == 0:
                            chain_a(qpos[i] + 1)
                        elif r == 1:
                            chain_b(qpos[i] + 1)
                        elif r == 2:
                            chain_c(qpos[i] + 1)
                    if r == 0:
                        sc_ = nxt('pcb', 4)
                        s1, pcb = 'PA%d' % (sc_ % 2), PA[sc_ % 2][:, (sc_ // 2) * 128:(sc_ // 2) * 128 + 128]
                        cbs[i] = (s1, pcb)
                        S.op('pe', C('matmul', pcb, lhsT=bmT[:, tsl], rhs=cmT[:, tsl], start=True, stop=True), reads=['xbc'], writes=[s1])
                    s1, pcb = cbs[i]
                    bk = k % 3
                    bank3 = (PB[0], PB[1], PD)[bk]
                    pA, pBm = bank3[:, 0:128], bank3[:, 128:256]
                    bkey = ('PB0', 'PB1', 'PD')[bk]
                    dh = dahi[:, i, hd:hd + 1].broadcast_to([128, 128])
                    dl = dalo[:, i, hd:hd + 1].broadcast_to([128, 128])
                    S.op('pe', C('matmul', pA, lhsT=dh, rhs=TRIB[d], start=True, stop=False)
                         + C('matmul', pA, lhsT=dl, rhs=TRIB[d], start=False, stop=False)
                         + C('matmul', pA, lhsT=identB[:], rhs=MNEGB[d], start=False, stop=True)
                         + C('matmul', pBm, lhsT=dh, rhs=TRIB[d], start=True, stop=False)
                         + C('matmul', pBm, lhsT=dl, rhs=TRIB[d], start=False, stop=True), reads=['dahi', 'dalo', 'cstB', 'identB'], writes=[bkey])
                    b3 = k % 4
                    S.op('act', C('activation', out=Lt[b3], in_=pA, func=AF.Exp, bias=ncumT[:, i, hd:hd + 1]), reads=[bkey, 'ncumT'], writes=['Lt%d' % b3])
                    S.op('act', C('activation', out=Et[b3], in_=pBm, func=AF.Exp), reads=[bkey], writes=['Et%d' % b3])
                    S.op('dve', C('tensor_tensor', out=Wt[b3], in0=pcb, in1=Lt[b3], op=ALU.mult), reads=[s1, 'Lt%d' % b3], writes=['Wt%d' % b3])
                    S.op('pool', C('tensor_tensor', out=Vt[b3], in0=Et[b3], in1=cmT[:, tsl], op=ALU.mult), reads=['Et%d' % b3, 'xbc'], writes=['Vt%d' % b3])

                def stage2(k, i, r):
                    tsl = slice(i * 128, (i + 1) * 128)
                    j, r2 = r // 2, r % 2
                    if r2 == 0:
                        ysl[(i, j)] = slot('pc')
                    s2, psy = ysl[(i, j)]
                    b3 = k % 4
                    pr = slice(r2 * 64, (r2 + 1) * 64)
                    cs = slice(r * 64, (r + 1) * 64)
                    S.op('pe', C('matmul', psy[pr, :], lhsT=xdtall[:, i, cs], rhs=Wt[b3], start=True, stop=False)
                         + C('matmul', psy[pr, :], lhsT=Ssball[:, i, cs], rhs=Vt[b3], start=False, stop=True),
                         reads=['xdt%d' % i, 'Wt%d' % b3, 'Vt%d' % b3, 'Ssb%d' % i], writes=[s2])
                    if r2 == 1:
                        yk = 'ya%d_%d' % (j, i)
                        if d == 0:
                            S.op('dve', C('scalar_tensor_tensor', out=yacc[:, j, tsl], in0=xsT[:, j, tsl], scalar=pc(PC_DSK + l * 8 + g * 2 + j), in1=psy, op0=ALU.mult, op1=ALU.add),
                                 reads=[s2, 'xbc', 'pcol'], writes=[yk])
                        else:
                            S.op('dve', C('tensor_tensor', out=yacc[:, j, tsl], in0=yacc[:, j, tsl], in1=psy, op=ALU.add), reads=[s2, yk], writes=[yk])
                pipeline(jobs, [lambda k, j: stage1(k, *j), lambda k, j: stage2(k, *j)], skew=3)
            S.tag = 'p3.fin.g%d' % g
            wz = [load_w(w_in[l, :, C_Z + g * 256 + j * 128:C_Z + g * 256 + (j + 1) * 128]) for j in range(2)]
            TG2 = [(q * 256, (q + 1) * 256) for q in range(T // 256)] if P3 >= 35 else []

            def g0(k, tg):
                t0, t1 = tg
                n = t1 - t0
                b = k % 2
                sqt, sqk = SQ2[b]
                pks = [inproj_fm(l, None, t0, t1, wz[j]) for j in range(2)]
                for j in range(2):
                    pk, ps = pks[j]
                    zb = acc2[j]
                    S.op('act', C('activation', out=zb[:, 0:n], in_=ps, func=AF.Exp, scale=-1.0), reads=[pk], writes=['acc%d' % j])
                    S.op('act', C('activation', out=zb[:, 0:n], in_=zb[:, 0:n], func=AF.Ln, bias=1.0), reads=['acc%d' % j], writes=['acc%d' % j])
                    S.op('act', C('activation', out=zb[:, 0:n], in_=zb[:, 0:n], func=AF.Exp, scale=-1.0), reads=['acc%d' % j], writes=['acc%d' % j])
                    S.op('dve', C('tensor_tensor', out=zb[:, 0:n], in0=ps, in1=zb[:, 0:n], op=ALU.mult), reads=[pk, 'acc%d' % j], writes=['acc%d' % j])
                    S.op('dve', C('tensor_tensor', out=y2[j][b][:, 0:n], in0=yacc[:, j, t0:t1], in1=zb[:, 0:n], op=ALU.mult),
                         reads=['acc%d' % j] + ['ya%d_%d' % (j, ti) for ti in range(t0 // 128, t1 // 128)], writes=['y2%d%d' % (j, b)])
                for j in range(2):
                    S.op('act', C('activation', out=sqt[:, j * 256:j * 256 + n], in_=y2[j][b][:, 0:n], func=AF.Square), reads=['y2%d%d' % (j, b)], writes=sqk)
                S.op('pe', C('matmul', PC[b][:, 0:n], lhsT=onesF, rhs=sqt[:, 0:n], start=True, stop=False)
                     + C('matmul', PC[b][:, 0:n], lhsT=onesF, rhs=sqt[:, 256:256 + n], start=False, stop=True), reads=sqk + ['cst'], writes=['PC%d' % b])

            def g1(k, tg):
                t0, t1 = tg
                n = t1 - t0
                b = k % 2
                sqt, sqk = SQ2[b]
                S.op('act', C('activation', out=sqt[:, 0:n], in_=PC[b][:, 0:n], func=AF.Ln, scale=1.0 / 256, bias=epsR[:, 0:1]), reads=['PC%d' % b, 'epsR'], writes=sqk)
                S.op('act', C('activation', out=sqt[:, 0:n], in_=sqt[:, 0:n], func=AF.Exp, scale=-0.5), reads=sqk, writes=sqk)
                for j in range(2):
                    S.op('dve', C('scalar_tensor_tensor', out=ybT[:, g * 2 + j, t0:t1], in0=y2[j][b][:, 0:n], scalar=pc(PC_BNW + l * 8 + g * 2 + j),
                                  in1=sqt[:, 0:n], op0=ALU.mult, op1=ALU.mult), reads=['y2%d%d' % (j, b), 'pcol'] + sqk, writes=['ybT'])
            S.tag = 'p3.fin.g%d' % g
            pipeline(TG2, [g0, g1])
        if l == 0:
            dump("ybT", ybT[:], ['ybT'])

        newphase()
        S.tag = 'p4.a'
        mg = carve([8, T], BF16)
        wo_sb = carve([8, D], BF16)
        wpa_c = carve([4, 128], BF16)
        wpb_c = carve([8, 128], BF16)
        sa = carve([512], F32)
        sb_ = carve([512], F32)
        gB = [carve([D], F32), carve([D], F32)]
        lnG = carve([D], F32)
        lnB = carve([D], F32)
        xt4 = [carve([D], F32), carve([D], F32)]
        vv = [carve([D], F32), carve([D], F32)]
        st6 = [carve([2, 6], F32), carve([2, 6], F32)]
        mv = [carve([4], F32), carve([4], F32)]
        for k in range(8):
            for hf in range(2):
                S.dma('pool', C('dma_start', out=wo_sb[:, k, hf * 512:(hf + 1) * 512], in_=w_o[l, k * 128:(k + 1) * 128, hf * 512:(hf + 1) * 512]), writes=['wo_sb'])
        S.dma('sp', C('dma_start', out=lnG, in_=ln_g[l:l + 1, :].partition_broadcast(128)), writes=['lnG'])
        S.dma('sp', C('dma_start', out=lnB, in_=ln_b[l:l + 1, :].partition_broadcast(128)), writes=['lnB'])
        for w in range(2):
            bcast_tile(gB[w], 'gB%d' % w, 2, w)
        for dc in range(8):
            wga = load_w(w_in[l, :, C_GA + dc * 128:C_GA + (dc + 1) * 128])
            wgb = load_w(w_in[l, :, C_GB + dc * 128:C_GB + (dc + 1) * 128])
            S.dma('pool', C('dma_start', out=wpa_c, in_=w_pa[l, :, dc * 128:(dc + 1) * 128].rearrange("(k p) c -> p k c", p=128)), writes=['wpa_c'])
            S.dma('pool', C('dma_start', out=wpb_c, in_=w_pb[l, :, dc * 128:(dc + 1) * 128].rearrange("(k p) c -> p k c", p=128)), writes=['wpb_c'])
            for (t0, t1) in TG:
                n = t1 - t0
                pk, ps = inproj_fm(l, None, t0, t1, wga)
                S.op('act', C('activation', out=sa[:, 0:n], in_=ps, func=AF.Sigmoid), reads=[pk], writes=['sa'])
                pk, ps = inproj_fm(l, None, t0, t1, wgb)
                S.op('act', C('activation', out=sb_[:, 0:n], in_=ps, func=AF.Sigmoid), reads=[pk], writes=['sb'])
                S.op('pe', MM([(PC[0][:, 0:n], wpa_c[:, k, :], yaT[:, k, t0:t1]) for k in range(4)]), reads=['wpa_c', 'yaT'], writes=['PC0'])
                S.op('pe', MM([(PC[1][:, 0:n], wpb_c[:, k, :], ybT[:, k, t0:t1]) for k in range(8)]), reads=['wpb_c', 'ybT'], writes=['PC1'])
                S.op('dve', C('tensor_tensor', out=sa[:, 0:n], in0=PC[0][:, 0:n], in1=sa[:, 0:n], op=ALU.mult), reads=['sa', 'PC0'], writes=['sa'])
                S.op('dve', C('tensor_tensor', out=sb_[:, 0:n], in0=PC[1][:, 0:n], in1=sb_[:, 0:n], op=ALU.mult), reads=['sb', 'PC1'], writes=['sb'])
                S.op('pool', C('tensor_tensor', out=mg[:, dc, t0:t1], in0=sa[:, 0:n], in1=sb_[:, 0:n], op=ALU.add), reads=['sa', 'sb'], writes=['mg'])
        if l + 1 < nlayers:
            emit_mod(l + 1)
        S.tag = 'p4.b'
        tiles4 = [i for i in range(NT) if not (last and i < 2)]

        def b0(k, i):
            b = k % 2
            w = 1 if i < 2 else 0
            tsl = slice(i * 128, (i + 1) * 128)
            if l == 0:
                src_ = ctx_in[i * 128:(i + 1) * 128, :] if i < 2 else x_in[(i - 2) * 128:(i - 1) * 128, :]
            else:
                src_ = xres[i * 128:(i + 1) * 128, :]
            S.dma('sp', C('dma_start', out=xt4[b], in_=src_), reads=['xres%d' % i], writes=['xt4%d' % b])
            banks = (PC, PB)[b]
            bkeys = (('PC0', 'PC1'), ('PB0', 'PB1'))[b]
            for hf in range(2):
                S.op('pe', MM([(banks[hf][:, :], mg[:, kk_, tsl], wo_sb[:, kk_, hf * 512:(hf + 1) * 512]) for kk_ in range(8)]),
                     reads=['mg', 'wo_sb'], writes=[bkeys[hf]])
                S.op('dve', C('tensor_tensor', out=vv[b][:, hf * 512:(hf + 1) * 512], in0=banks[hf][:, :], in1=gB[w][:, hf * 512:(hf + 1) * 512], op=ALU.mult),
                     reads=['gB%d' % w, bkeys[hf]], writes=['vv%d' % b])

        def b1(k, i):
            b = k % 2
            vk, mk, sk = 'vv%d' % b, 'mv%d' % b, 'st6%d' % b
            v_, m_, s_ = vv[b], mv[b], st6[b]
            S.op('dve', C('scalar_tensor_tensor', out=v_, in0=xt4[b], scalar=float(DN_ALPHA), in1=v_, op0=ALU.mult, op1=ALU.add), reads=['xt4%d' % b, vk], writes=[vk])
            S.op('dve', C('bn_stats', out=s_[:, 0, :], in_=v_[:, 0:512]), reads=[vk], writes=[sk])
            S.op('dve', C('bn_stats', out=s_[:, 1, :], in_=v_[:, 512:1024]), reads=[vk], writes=[sk])
            S.op('dve', C('bn_aggr', out=m_[:, 0:2], in_=s_[:, :, :]), reads=[sk], writes=[mk])
            S.op('act', C('activation', out=m_[:, 2:3], in_=m_[:, 1:2], func=AF.Ln, bias=epsR[:, 1:2]), reads=[mk, 'epsR'], writes=[mk])
            S.op('act', C('activation', out=m_[:, 2:3], in_=m_[:, 2:3], func=AF.Exp, scale=-0.5), reads=[mk], writes=[mk])
            S.op('dve', C('scalar_tensor_tensor', out=m_[:, 3:4], in0=m_[:, 0:1], scalar=-1.0, in1=m_[:, 2:3], op0=ALU.mult, op1=ALU.mult), reads=[mk], writes=[mk])
            S.op('act', C('activation', out=v_, in_=v_, func=AF.Identity, scale=m_[:, 2:3], bias=m_[:, 3:4]), reads=[vk, mk], writes=[vk])
            S.op('dve', C('tensor_tensor', out=v_, in0=v_, in1=lnG, op=ALU.mult), reads=[vk, 'lnG'], writes=[vk])
            S.op('pool', C('tensor_tensor', out=xt4[b], in0=v_, in1=lnB, op=ALU.add), reads=[vk, 'lnB', 'xt4%d' % b], writes=['xt4%d' % b])
            dst = out[(i - 2) * 128:(i - 1) * 128, :] if last else xres[i * 128:(i + 1) * 128, :]
            S.dma('sp', C('dma_start', out=dst, in_=xt4[b]), reads=['xt4%d' % b], writes=['xres%d' % i])
        pipeline(tiles4, [b0, b1])
    if dbg and nlayers < DEPTH:
        S.barrier()
        dres = nc.dram_tensor("dbg_xres", [T, D], F32, kind="ExternalOutput").ap()
        dbg_outs["xres"] = [T, D]
        S.dma('sp', C('dma_start', out=dres, in_=xres), reads=['xres%d' % i for i in range(NT)])
    with nc.Block() as block:
        S.emit(block)
    return nc, dbg_outs


def make_consts():
    c = np.zeros((128, K_N), np.float32)
    s = np.arange(128)[:, None]
    t = np.arange(128)[None, :]
    c[:, K_ID:K_ID + 128] = np.eye(128)
    c[:, K_ONE:K_ONE + 128] = 1.0
    c[:, K_TRIF:K_TRIF + 128] = (s <= t)
    c[:, K_TRIB:K_TRIB + 128] = (s >= t)
    c[:, K_MNF:K_MNF + 128] = np.where(s <= t, 0.0, NEG)
    c[:, K_MNB:K_MNB + 128] = np.where(s >= t, 0.0, NEG)
    same = (s // 64) == (t // 64)
    c[:, K_HMF:K_HMF + 128] = (s <= t) & same
    c[:, K_HMB:K_HMB + 128] = (s >= t) & same
    rm = np.ones(512, np.float32)
    rm[::64] = 0.0
    c[:, K_RM:K_RM + 512] = rm[None, :]
    return c


def pack_cols(inp, b):
    pcv = np.zeros((128, PC_N), np.float32)
    lbl = np.asarray(inp['a_lb_logits'], np.float32).reshape(2, 4, 4, 128)
    pcv[:, PC_LBL:PC_LBL + 32] = lbl.transpose(3, 0, 1, 2).reshape(128, 32)
    pcv[:, PC_ANW:PC_ANW + 4] = np.asarray(inp['a_norm_w'], np.float32).T
    cw = np.asarray(inp['b_conv_w'], np.float32).reshape(4, 5, 16, 128)
    pcv[:, PC_CW:PC_CW + 320] = cw.transpose(3, 0, 2, 1).reshape(128, 320)
    cb = np.asarray(inp['b_conv_b'], np.float32).reshape(4, 16, 128)
    pcv[:, PC_CB:PC_CB + 64] = cb.transpose(2, 0, 1).reshape(128, 64)
    dsk = np.repeat(np.asarray(inp['b_d'], np.float32), 64, axis=1).reshape(4, 8, 128)
    pcv[:, PC_DSK:PC_DSK + 32] = dsk.transpose(2, 0, 1).reshape(128, 32)
    bnw = np.asarray(inp['b_norm_w'], np.float32).reshape(4, 8, 128)
    pcv[:, PC_BNW:PC_BNW + 32] = bnw.transpose(2, 0, 1).reshape(128, 32)
    bm = np.asarray(inp['b_mod'], np.float32).reshape(4, 24, 128)
    pcv[:, PC_BMOD:PC_BMOD + 96] = bm.transpose(2, 0, 1).reshape(128, 96)
    pcv[:, PC_C:PC_C + 8] = np.asarray(inp['c'], np.float32)[b].reshape(8, 128).T
    pcv[:, PC_C + 8:PC_C + 16] = np.asarray(inp['c_ctx'], np.float32).reshape(8, 128).T
    prow = np.concatenate([np.asarray(inp['b_dt_bias'], np.float32).reshape(-1), np.asarray(inp['b_a_log'], np.float32).reshape(-1)])[None, :]
    return pcv, np.ascontiguousarray(prow)


_CACHE = {}


def kernel(**inputs):
    if 'nc' not in _CACHE:
        _CACHE['nc'] = build()[0]
    nc = _CACHE['nc']
    cstv = make_consts()
    f = lambda k: np.ascontiguousarray(np.asarray(inputs[k], np.float32))
    shared = {k: f(k) for k in ['w_mod', 'w_in', 'w_proj_a', 'w_proj_b', 'w_out', 'ln_g', 'ln_b']}
    x = f('x')
    ctx = f('ctx')
    in_maps = []
    for b in range(8):
        pcv, prow = pack_cols(inputs, b)
        m = dict(shared)
        m.update({'x': x[b], 'ctx': ctx[b], 'pcol_d': pcv, 'prow_d': prow, 'cst_d': cstv})
        in_maps.append(m)
    res = run_bass_kernel_spmd(nc, in_maps, core_ids=list(range(8)))
    return np.stack([np.asarray(r['out'], np.float32) for r in res.results], axis=0)
```

```python
import numpy as np
import ml_dtypes
import concourse.bass as bass
import concourse.mybir as mybir
from concourse.bass_utils import run_bass_kernel_spmd

F32 = mybir.dt.float32
BF16 = mybir.dt.bfloat16
AF = mybir.ActivationFunctionType
ALU = mybir.AluOpType

D = 1024
DEPTH = 4
CTX = 256
SEQ = 2048
T = CTX + SEQ
NT = T // 128
INC = 7712
C_Q, C_FF, C_FB, C_I, C_G, C_Z, C_XS, C_BM, C_CM, C_DT, C_GA, C_GB = 0, 512, 1024, 1536, 2048, 2560, 3584, 4608, 5120, 5632, 5664, 6688
TG = [(0, 256)] + [(256 + 512 * i, 256 + 512 * (i + 1)) for i in range(4)]
DN_ALPHA = (2 * DEPTH) ** 0.25
LN_EPS = 1e-5
RMS_EPS = 1e-6
BIG = 1e17
NEG = -30000.0

PC_LBL = 0
PC_ANW = PC_LBL + 32
PC_CW = PC_ANW + 4
PC_CB = PC_CW + 320
PC_DSK = PC_CB + 64
PC_BNW = PC_DSK + 32
PC_BMOD = PC_BNW + 32
PC_C = PC_BMOD + 96
PC_N = PC_C + 16
K_ID, K_ONE, K_TRIF, K_TRIB, K_MNF, K_MNB, K_HMF, K_HMB, K_RM = [i * 128 for i in range(9)]
K_N = K_RM + 512


class Sched:
    ENG = ['pe', 'act', 'dve', 'pool', 'sp']

    def __init__(self, nc, ndma_sems=8):
        self.nc = nc
        self.q = {e: [] for e in self.ENG}
        self.sem = {e: nc.alloc_semaphore("s_" + e) for e in ('pe', 'act', 'dve', 'pool')}
        self.count = {e: 0 for e in self.ENG}
        self.waited = {e: {} for e in self.ENG}
        self.last_write = {}
        self.readers = {}
        self.dsems = {e: [nc.alloc_semaphore("d_%s%d" % (e, i)) for i in range(ndma_sems)] for e in ('sp', 'pool')}
        self.dcnt = {e: [0] * ndma_sems for e in self.dsems}
        self.drr = {e: 0 for e in self.dsems}
        self.tag = ''
        self.annotate = False

    def _deps(self, eng, reads, writes):
        need = {}

        def add(tok):
            if tok is None:
                return
            key, val = tok
            if need.get(key, 0) < val:
                need[key] = val
        for r in reads:
            add(self.last_write.get(r))
        for w in writes:
            add(self.last_write.get(w))
            for t in self.readers.get(w, ()):
                add(t)
        waits = []
        for key, val in need.items():
            if eng == 'pe' and key == ('c', 'pe'):
                continue
            if self.waited[eng].get(key, 0) < val:
                self.waited[eng][key] = val
                waits.append((key, val))
        return waits

    def _commit(self, tok, reads, writes):
        for r in reads:
            lst = self.readers.setdefault(r, [])
            for i, (k, v) in enumerate(lst):
                if k == tok[0]:
                    lst[i] = tok
                    break
            else:
                lst.append(tok)
        for w in writes:
            self.last_write[w] = tok
            self.readers[w] = []

    def op(self, eng, fn, reads=(), writes=()):
        writes = list(writes) + [r for r in reads if r[0] == 'P']
        reads = [r for r in reads if r[0] != 'P']
        waits = self._deps(eng, reads, writes)
        self.count[eng] += 1
        tok = (('c', eng), self.count[eng])
        self.q[eng].append((waits, fn, (self.sem[eng], 1), self.tag))
        self._commit(tok, reads, writes)

    def dma(self, eng, fn, reads=(), writes=()):
        waits = self._deps(eng, reads, writes)
        k = self.drr[eng]
        self.drr[eng] = (k + 1) % len(self.dsems[eng])
        prev = self.dcnt[eng][k]
        key = ('d', eng, k)
        if prev and self.waited[eng].get(key, 0) < prev:
            self.waited[eng][key] = prev
            waits.append((key, prev))
        self.dcnt[eng][k] = prev + 16
        tok = (key, prev + 16)
        self.q[eng].append((waits, fn, (self.dsems[eng][k], 16), self.tag))
        self._commit(tok, reads, writes)

    def all_tokens(self):
        toks = []
        for e in ('pe', 'act', 'dve', 'pool'):
            if self.count[e]:
                toks.append((('c', e), self.count[e]))
        for e in self.dsems:
            for k, v in enumerate(self.dcnt[e]):
                if v:
                    toks.append((('d', e, k), v))
        return toks

    def barrier(self):
        toks = self.all_tokens()
        for eng in self.ENG:
            waits = []
            for key, val in toks:
                if key == ('c', eng):
                    continue
                if self.waited[eng].get(key, 0) < val:
                    self.waited[eng][key] = val
                    waits.append((key, val))
            if waits:
                self.q[eng].append((waits, None, None, self.tag))

    def _semof(self, key):
        if key[0] == 'c':
            return self.sem[key[1]]
        return self.dsems[key[1]][key[2]]

    def emit(self, block):
        S = self
        self.barrier()

        def run(ename, e):
            for waits, fn, si, tag in S.q[ename]:
                for key, val in waits:
                    e.wait_ge(S._semof(key), val)
                if fn is not None:
                    ins = None
                    for name, a, k in fn:
                        ins = getattr(e, name)(*a, **k)
                        if S.annotate:
                            ins.annotate(tag)
                    ins.then_inc(si[0], si[1])

        @block.tensor
        def _(e):
            run('pe', e)

        @block.scalar
        def _(e):
            run('act', e)

        @block.vector
        def _(e):
            run('dve', e)

        @block.gpsimd
        def _(e):
            run('pool', e)

        @block.sync
        def _(e):
            run('sp', e)


def pipeline(items, stages, skew=1):
    n = len(items)
    for t in range(n + (len(stages) - 1) * skew):
        for s, f in enumerate(stages):
            k = t - s * skew
            if 0 <= k < n:
                f(k, items[k])


def C(name, *a, **k):
    return [(name, a, k)]


def MM(items):
    n = len(items)
    return [('matmul', (o,), dict(lhsT=l, rhs=r, start=(i == 0), stop=(i == n - 1))) for i, (o, l, r) in enumerate(items)]


def build(nlayers=DEPTH, dbg=False, phases=(1, 2, 3, 4), annotate=False):
    nc = bass.Bass("TRN2", target_bir_lowering=False)
    dt_in = lambda n, s, d=F32: nc.dram_tensor(n, s, d, kind="ExternalInput").ap()
    x_in = dt_in("x", [SEQ, D])
    ctx_in = dt_in("ctx", [CTX, D])
    w_mod = dt_in("w_mod", [DEPTH, D, 3 * D])
    w_in = dt_in("w_in", [DEPTH, D, INC])
    w_pa = dt_in("w_proj_a", [DEPTH, 512, D])
    w_pb = dt_in("w_proj_b", [DEPTH, D, D])
    w_o = dt_in("w_out", [DEPTH, D, D])
    ln_g = dt_in("ln_g", [DEPTH, D])
    ln_b = dt_in("ln_b", [DEPTH, D])
    pcol_in = dt_in("pcol_d", [128, PC_N])
    prow_in = dt_in("prow_d", [1, 256])
    cst_in = dt_in("cst_d", [128, K_N])
    out = nc.dram_tensor("out", [SEQ, D], F32, kind="ExternalOutput").ap()
    xres = nc.dram_tensor("xres", [T, D], F32).ap()
    dbg_outs = {}

    S = Sched(nc)
    S.annotate = annotate

    cst = nc.alloc_sbuf_tensor("cst", [128, K_N], F32)
    pcol = nc.alloc_sbuf_tensor("pcol", [128, PC_N], F32)
    prowB = nc.alloc_sbuf_tensor("prowB", [128, 256], F32)
    anegB = nc.alloc_sbuf_tensor("anegB", [128, 128], F32)
    lbs = nc.alloc_sbuf_tensor("lbs", [128, 5, 32], F32)
    identB = nc.alloc_sbuf_tensor("identB", [128, 128], BF16)
    cstB = nc.alloc_sbuf_tensor("cstB", [128, 4, 128], BF16)
    hT = nc.alloc_sbuf_tensor("hT", [128, 8, T], BF16)
    yaT = nc.alloc_sbuf_tensor("yaT", [128, 4, T], BF16)
    ybT = nc.alloc_sbuf_tensor("ybT", [128, 8, T], BF16)
    modT2 = [nc.alloc_sbuf_tensor("modT%d" % i, [128, 24, 2], F32) for i in range(2)]
    cact = nc.alloc_sbuf_tensor("cact", [128, 16], BF16)
    epsR = nc.alloc_sbuf_tensor("epsR", [128, 2], F32)
    NW = 4
    wbuf = [nc.alloc_sbuf_tensor("wbuf%d" % i, [128, 8, 128], BF16) for i in range(NW)]
    wrr = [0]
    nbig = (nc.sbuf_bytes_remaining - 512) // 4
    big = nc.alloc_sbuf_tensor("big", [128, nbig], F32)
    cur = [0]

    def carve(shape, dtype):
        n = int(np.prod(shape))
        words = (n + 1) // 2 if dtype == BF16 else n
        words = (words + 7) // 8 * 8
        a = big[:, cur[0]:cur[0] + words]
        cur[0] += words
        assert cur[0] <= nbig, ("SBUF phase scratch overflow", cur[0], nbig)
        if dtype == BF16:
            a = a.bitcast(BF16)[:, 0:n]
        else:
            a = a[:, 0:n]
        if len(shape) == 2:
            return a.rearrange("p (a b) -> p a b", b=shape[1])
        if len(shape) == 3:
            return a.rearrange("p (a b c) -> p a b c", b=shape[1], c=shape[2])
        return a

    def newphase():
        S.barrier()
        cur[0] = 0

    PA = [nc.alloc_psum_tensor("PA%d" % i, [128, 512], F32) for i in range(2)]
    PB = [nc.alloc_psum_tensor("PB%d" % i, [128, 512], F32) for i in range(2)]
    PC = [nc.alloc_psum_tensor("PC%d" % i, [128, 512], F32) for i in range(2)]
    PD = nc.alloc_psum_tensor("PD", [128, 512], F32)
    PT = nc.alloc_psum_tensor("PT", [128, 1024], BF16)
    rr = {'pa': 0, 'pb': 0, 'pc': 0, 'pd': 0, 'pai': 0, 'pcb': 0}

    def tbank(i):
        if i % 2 == 0:
            return 'PT', PT[:, :]
        return 'PD', PD[:, :].bitcast(BF16)

    def nxt(kind, n):
        v = rr[kind]
        rr[kind] = (v + 1) % n
        return v

    def slot(kind):
        banks = {'pa': PA, 'pb': PB, 'pc': PC}[kind]
        s = nxt(kind, 8)
        b, c = s % 2, (s // 2) * 128
        return '%s%d' % (kind.upper(), b), banks[b][:, c:c + 128]

    identF = cst[:, K_ID:K_ID + 128]
    onesF = cst[:, K_ONE:K_ONE + 128]
    TRI = [cst[:, K_TRIF:K_TRIF + 128], cst[:, K_TRIB:K_TRIB + 128]]
    MNEG = [cst[:, K_MNF:K_MNF + 128], cst[:, K_MNB:K_MNB + 128]]
    HMASK = [cst[:, K_HMF:K_HMF + 128], cst[:, K_HMB:K_HMB + 128]]
    RMASK = cst[:, K_RM:K_RM + 512]

    def pc(off, n=1):
        return pcol[:, off:off + n]

    def load_w(src_ap, kchunks=8, ncols=128):
        i = wrr[0]
        wrr[0] = (i + 1) % NW
        key = 'w%d' % i
        dst = wbuf[i][:, 0:kchunks, 0:ncols]
        S.dma('pool', C('dma_start', out=dst, in_=src_ap.rearrange("(k p) c -> p k c", p=128)), writes=[key])
        return key, wbuf[i]

    def inproj_fm(l, col0, t0, t1, wk=None):
        if wk is None:
            wk = load_w(w_in[l, :, col0:col0 + 128])
        wkey, wt = wk
        i = nxt('pai', 2)
        pskey = 'PA%d' % i
        ps = PA[i][:, 0:t1 - t0]
        S.op('pe', MM([(ps, wt[:, k, :], hT[:, k, t0:t1]) for k in range(8)]), reads=[wkey, 'hT'], writes=[pskey])
        return pskey, ps

    def dump(name, ap, reads):
        if not dbg:
            return
        shp = list(ap.shape)
        d = nc.dram_tensor("dbg_" + name, shp, ap.dtype, kind="ExternalOutput").ap()
        dbg_outs[name] = shp
        S.dma('sp', C('dma_start', out=d, in_=ap), reads=reads)

    S.dma('sp', C('dma_start', out=cst[:], in_=cst_in), writes=['cst'])
    S.dma('sp', C('dma_start', out=pcol[:], in_=pcol_in), writes=['pcol'])
    S.dma('sp', C('dma_start', out=prowB[:], in_=prow_in.partition_broadcast(128)), writes=['prowB'])
    S.op('dve', C('tensor_copy', out=identB[:], in_=identF), reads=['cst'], writes=['identB'])
    S.op('dve', C('tensor_copy', out=cstB[:], in_=cst[:, K_TRIF:K_TRIF + 512].rearrange("p (a b) -> p a b", b=128)), reads=['cst'], writes=['cstB'])
    TRIB = [cstB[:, 0, :], cstB[:, 1, :]]
    MNEGB = [cstB[:, 2, :], cstB[:, 3, :]]
    S.op('pool', C('memset', epsR[:, 0:1], RMS_EPS), writes=['epsR'])
    S.op('pool', C('memset', epsR[:, 1:2], LN_EPS), reads=[], writes=['epsR'])
    S.op('act', C('activation', out=anegB[:], in_=prowB[:, 128:256], func=AF.Exp), reads=['prowB'], writes=['anegB'])
    S.op('act', C('mul', out=anegB[:], in_=anegB[:], mul=-1.0), reads=['anegB'], writes=['anegB'])
    S.op('act', C('activation', out=cact[:], in_=pc(PC_C, 16), func=AF.Silu), reads=['pcol'], writes=['cact'])
    lb4 = lambda j: lbs[:, j, :].rearrange("p (d l a) -> p d l a", d=2, l=4)
    S.op('act', C('activation', out=lbs[:, 0, :], in_=pc(PC_LBL, 32), func=AF.Exp), reads=['pcol'], writes=['lbs'])
    E4 = lb4(0)
    tmp4 = lb4(4)
    S.op('dve', C('tensor_tensor', out=tmp4[:, :, 0, :], in0=E4[:, :, 0, :], in1=E4[:, :, 1, :], op=ALU.add), reads=['lbs'], writes=['lbs'])
    S.op('dve', C('tensor_tensor', out=tmp4[:, :, 0, :], in0=tmp4[:, :, 0, :], in1=E4[:, :, 2, :], op=ALU.add), reads=['lbs'], writes=['lbs'])
    S.op('dve', C('tensor_tensor', out=tmp4[:, :, 0, :], in0=tmp4[:, :, 0, :], in1=E4[:, :, 3, :], op=ALU.add), reads=['lbs'], writes=['lbs'])
    S.op('dve', C('reciprocal', out=tmp4[:, :, 1, :], in_=tmp4[:, :, 0, :]), reads=['lbs'], writes=['lbs'])
    for l in range(1, 4):
        S.op('dve', C('tensor_tensor', out=E4[:, :, l, :], in0=E4[:, :, l, :], in1=tmp4[:, :, 1, :], op=ALU.mult), reads=['lbs'], writes=['lbs'])
    LO = lb4(1)
    S.op('dve', C('memset', LO[:, :, 0, :], 0.0), reads=['lbs'], writes=['lbs'])
    S.op('dve', C('tensor_copy', out=LO[:, :, 1, :], in_=E4[:, :, 1, :]), reads=['lbs'], writes=['lbs'])
    S.op('dve', C('tensor_tensor', out=LO[:, :, 2, :], in0=LO[:, :, 1, :], in1=E4[:, :, 2, :], op=ALU.add), reads=['lbs'], writes=['lbs'])
    S.op('dve', C('tensor_tensor', out=LO[:, :, 3, :], in0=LO[:, :, 2, :], in1=E4[:, :, 3, :], op=ALU.add), reads=['lbs'], writes=['lbs'])
    S.op('dve', C('tensor_scalar', out=lbs[:, 2, :], in0=lbs[:, 1, :], scalar1=-1.0, scalar2=1.0, op0=ALU.mult, op1=ALU.add), reads=['lbs'], writes=['lbs'])
    S.op('dve', C('tensor_scalar', out=lbs[:, 3, :], in0=lbs[:, 1, :], scalar1=-1.0, scalar2=None, op0=ALU.add), reads=['lbs'], writes=['lbs'])

    def lbcol(j, d, l, a):
        o = d * 16 + l * 4 + a
        return lbs[:, j, o:o + 1]

    def emit_mod(ll):
        S.tag = 'mod'
        mT = modT2[ll % 2]
        mk = 'modT%d' % (ll % 2)
        for nb in range(6):
            wks = []
            for k2 in range(2):
                pass
            for cc in range(4):
                ch = nb * 4 + cc
                wkey, wt = load_w(w_mod[ll, :, ch * 128:(ch + 1) * 128])
                ps = PD[:, ch * 2:ch * 2 + 2]
                S.op('pe', MM([(ps, wt[:, k, :], cact[:, k:k + 9:8]) for k in range(8)]),
                     reads=[wkey, 'cact'], writes=['PD'])
        S.op('dve', C('tensor_tensor', out=mT[:], in0=PD[:, 0:48].rearrange("p (c w) -> p c w", w=2),
                                              in1=pc(PC_BMOD + ll * 24, 24).unsqueeze(2).broadcast_to([128, 24, 2]), op=ALU.add),
             reads=['PD', 'pcol'], writes=[mk])
        S.op('dve', C('tensor_scalar', out=mT[:, 8:16, :], in0=mT[:, 8:16, :], scalar1=1.0, scalar2=None, op0=ALU.add), reads=[mk], writes=[mk])


    for l in range(nlayers):
        last = (l == DEPTH - 1)
        newphase()
        if l == 0:
            emit_mod(0)
        modT = modT2[l % 2]
        MK = 'modT%d' % (l % 2)
        def bcast_tile(dst, dkey, part, which):
            for half in range(2):
                for jj in range(4):
                    j = half * 4 + jj
                    S.op('pe', C('matmul', PD[:, jj * 128:(jj + 1) * 128], lhsT=modT[:, part * 8 + j, which:which + 1].broadcast_to([128, 128]),
                                                              rhs=identF, start=True, stop=True), reads=[MK, 'cst'], writes=['PD'])
                S.op('act', C('copy', out=dst[:, half * 512:(half + 1) * 512], in_=PD[:, :]), reads=['PD'], writes=[dkey])

        shB = [carve([D], F32), carve([D], F32)]
        scB = [carve([D], F32), carve([D], F32)]
        for w in range(2):
            bcast_tile(shB[w], 'shB%d' % w, 0, w)
            bcast_tile(scB[w], 'scB%d' % w, 1, w)
        S.tag = 'p1'
        xt = [carve([D], F32), carve([D], F32)]
        h1 = carve([D], F32)
        hb = [carve([D], BF16), carve([D], BF16)]
        for i in range(NT):
            w = 1 if i < 2 else 0
            b = i % 2
            if l == 0:
                src = ctx_in[i * 128:(i + 1) * 128, :] if i < 2 else x_in[(i - 2) * 128:(i - 1) * 128, :]
            else:
                src = xres[i * 128:(i + 1) * 128, :]
            S.dma('sp', C('dma_start', out=xt[b], in_=src), reads=['xres%d' % i], writes=['xt%d' % b])
            S.op('dve', C('tensor_tensor', out=h1, in0=xt[b], in1=scB[w], op=ALU.mult), reads=['xt%d' % b, 'scB%d' % w], writes=['h1'])
            S.op('pool', C('tensor_tensor', out=hb[b], in0=h1, in1=shB[w], op=ALU.add), reads=['h1', 'shB%d' % w], writes=['hb%d' % b])
            tk_, tb_ = tbank(i)
            S.op('pe', sum([C('transpose', out=tb_[:, k * 128:(k + 1) * 128], in_=hb[b][:, k * 128:(k + 1) * 128], identity=identB[:]) for k in range(8)], []),
                 reads=['hb%d' % b, 'identB'], writes=[tk_])
            S.op('act', C('copy', out=hT[:, :, i * 128:(i + 1) * 128], in_=tb_.rearrange("p (k t) -> p k t", t=128)), reads=[tk_], writes=['hT'])

        newphase()
        HEADS = (range(4) if 2 in phases else []) if 22 not in phases else [0]
        qT2 = [carve([T], BF16), carve([T], BF16)]
        vtok2 = [carve([NT, 128], BF16), carve([NT, 128], BF16)]
        qe = carve([T], BF16)
        ke = carve([T], BF16)
        keT = carve([NT, 128], BF16)
        oacc = carve([T], F32)
        dec = carve([36], F32)
        em = carve([36], F32)
        etm = carve([36], F32)
        Sst2 = [carve([128], F32) for _ in range(4)]
        Ue = [carve([128], F32) for _ in range(4)]
        Sball = carve([36, 128], BF16)
        scm = [carve([128], BF16) for _ in range(4)]
        sg = [carve([512], F32) for _ in range(2)]
        lf = [carve([512], F32) for _ in range(2)]
        kk = [carve([512], F32) for _ in range(3)]
        cum = [carve([512], F32) for _ in range(2)]
        bb = [carve([512], F32) for _ in range(2)]
        dd = [carve([512], F32) for _ in range(2)]
        ee = [carve([512], F32) for _ in range(4)]
        gs = [carve([512], F32), carve([512], F32)]
        r1 = [carve([512], F32), carve([512], F32)]
        def emit_qv(a):
            qT, vtok = qT2[a % 2], vtok2[a % 2]
            QK, VK = 'qT%d' % (a % 2), 'vtok%d' % (a % 2)
            S.tag = 'p2.qv.a%d' % a
            wk = load_w(w_in[l, :, C_Q + a * 128:C_Q + (a + 1) * 128])
            for (t0, t1) in TG:
                pk, ps = inproj_fm(l, None, t0, t1, wk)
                S.op('act', C('copy', out=qT[:, t0:t1], in_=ps), reads=[pk], writes=[QK])
            wkey, wt = load_w(w_in[l, :, C_I + a * 128:C_I + (a + 1) * 128])
            for i in range(NT):
                sv, ps = slot('pb')
                S.op('pe', MM([(ps, hT[:, k, i * 128:(i + 1) * 128], wt[:, k, :]) for k in range(8)]),
                     reads=[wkey, 'hT'], writes=[sv])
                S.op('act', C('copy', out=vtok[:, i, :], in_=ps), reads=[sv], writes=[VK])
        HL = list(HEADS)
        if HL:
            emit_qv(HL[0])
        for hi, a in enumerate(HL):
            qT, vtok = qT2[a % 2], vtok2[a % 2]
            QK, VK = 'qT%d' % (a % 2), 'vtok%d' % (a % 2)
            for d in range(2):
                wk = load_w(w_in[l, :, (C_FF, C_FB)[d] + a * 128:(C_FF, C_FB)[d] + (a + 1) * 128])
                LBc, OMLc, NOMLc = lbcol(1, d, l, a), lbcol(2, d, l, a), lbcol(3, d, l, a)
                S.tag = 'p2.prep.a%dd%d' % (a, d)

                def v3(ap, n):
                    return ap[:, 0:n].rearrange("p (c t) -> p c t", t=64)

                def p0(k, tg):
                    t0, t1 = tg
                    n = t1 - t0
                    pk, ps = inproj_fm(l, None, t0, t1, wk)
                    S.op('act', C('activation', out=sg[k % 2][:, 0:n], in_=ps, func=AF.Sigmoid), reads=[pk], writes=['sg%d' % (k % 2)])
                    S.op('act', C('activation', out=lf[k % 2][:, 0:n], in_=sg[k % 2][:, 0:n], func=AF.Ln, scale=OMLc, bias=LBc), reads=['sg%d' % (k % 2), 'lbs'], writes=['lf%d' % (k % 2)])
                    S.op('pool', C('tensor_scalar', out=kk[k % 3][:, 0:n], in0=sg[k % 2][:, 0:n], scalar1=NOMLc, scalar2=OMLc, op0=ALU.mult, op1=ALU.add),
                         reads=['sg%d' % (k % 2), 'lbs'], writes=['kk%d' % (k % 3)])

                def p1(k, tg):
                    t0, t1 = tg
                    n = t1 - t0
                    ncn, c0 = n // 64, t0 // 64
                    cu, lfk = cum[k % 2], lf[k % 2]
                    ckey = 'cum%d' % (k % 2)
                    S.op('dve', C('tensor_tensor_scan', out=cu[:, 0:n], data0=RMASK[:, 0:n], data1=lfk[:, 0:n], initial=0.0, op0=ALU.mult, op1=ALU.add),
                         reads=['lf%d' % (k % 2), 'cst'], writes=[ckey])
                    tot1 = v3(cu, n)[:, :, 63:64]
                    if d == 0:
                        bsrc, bkey = cu, ckey
                    else:
                        bsrc, bkey = bb[k % 2], 'bb%d' % (k % 2)
                        S.op('pool', C('tensor_tensor', out=bsrc[:, 0:n], in0=cu[:, 0:n], in1=lfk[:, 0:n], op=ALU.subtract), reads=[ckey, 'lf%d' % (k % 2)], writes=[bkey])
                        S.op('dve', C('tensor_tensor', out=v3(bsrc, n), in0=tot1.broadcast_to([128, ncn, 64]), in1=v3(bsrc, n), op=ALU.subtract), reads=[ckey, bkey], writes=[bkey])
                    mid1 = v3(bsrc, n)[:, :, 32:33]
                    S.op('pool', C('tensor_tensor', out=v3(dd[k % 2], n), in0=v3(bsrc, n), in1=mid1.broadcast_to([128, ncn, 64]), op=ALU.subtract), reads=[bkey], writes=['dd%d' % (k % 2)])
                    S.op('act', C('activation', out=dec[:, c0:c0 + ncn].unsqueeze(2), in_=tot1, func=AF.Exp), reads=[ckey], writes=['dec'])
                    S.op('act', C('activation', out=em[:, c0:c0 + ncn].unsqueeze(2), in_=mid1, func=AF.Exp), reads=[bkey], writes=['em'])
                    S.op('dve', C('tensor_tensor', out=etm[:, c0:c0 + ncn].unsqueeze(2), in0=tot1, in1=mid1, op=ALU.subtract), reads=[ckey, bkey], writes=['etm'])
                    S.op('act', C('activation', out=etm[:, c0:c0 + ncn], in_=etm[:, c0:c0 + ncn], func=AF.Exp), reads=['etm'], writes=['etm'])

                def p2(k, tg):
                    t0, t1 = tg
                    n = t1 - t0
                    e1, e2 = ee[(2 * k) % 4], ee[(2 * k + 1) % 4]
                    k1, k2 = 'ee%d' % ((2 * k) % 4), 'ee%d' % ((2 * k + 1) % 4)
                    S.op('act', C('activation', out=e1[:, 0:n], in_=dd[k % 2][:, 0:n], func=AF.Exp), reads=['dd%d' % (k % 2)], writes=[k1])
                    S.op('act', C('activation', out=e2[:, 0:n], in_=dd[k % 2][:, 0:n], func=AF.Exp, scale=-1.0), reads=['dd%d' % (k % 2)], writes=[k2])
                    S.op('dve', C('scalar_tensor_tensor', out=qe[:, t0:t1], in0=e1[:, 0:n], scalar=BIG, in1=qT[:, t0:t1], op0=ALU.min, op1=ALU.mult),
                         reads=[k1, QK], writes=['qe'])
                    S.op('dve', C('scalar_tensor_tensor', out=ke[:, t0:t1], in0=e2[:, 0:n], scalar=BIG, in1=kk[k % 3][:, 0:n], op0=ALU.min, op1=ALU.mult),
                         reads=[k2, 'kk%d' % (k % 3)], writes=['ke'])
                    nb = n // 128
                    tk_, tb_ = tbank(k)
                    S.op('pe', sum([C('transpose', out=tb_[:, j * 128:(j + 1) * 128], in_=ke[:, t0 + j * 128:t0 + (j + 1) * 128], identity=identB[:]) for j in range(nb)], []),
                         reads=['ke', 'identB'], writes=[tk_])
                    S.op('act', C('copy', out=keT[:, t0 // 128:t0 // 128 + nb, :], in_=tb_[:, 0:nb * 128].rearrange("p (j k) -> p j k", k=128)),
                         reads=[tk_], writes=['keT'])
                pipeline(TG, [p0, p1, p2])
                if d == 0 and hi + 1 < len(HL):
                    emit_qv(HL[hi + 1])
                S.tag = 'p2.pass1.a%dd%d' % (a, d)
                order = list(range(NT)) if d == 0 else [1, 0] + list(range(NT - 1, 1, -1))
                corder = []
                for i in order:
                    for c in ((0, 1) if d == 0 else (1, 0)):
                        corder.append(i * 2 + c)
                S.op('dve', C('memset', Sst2[0], 0.0), writes=['Sst0'])
                S.op('pool', C('memset', Sball[:, corder[0], :], 0.0), writes=['Sb%d' % corder[0]])
                Ue4 = Ue

                def u0(q, ci):
                    i, c = ci // 2, ci % 2
                    pr = slice(c * 64, (c + 1) * 64)
                    s3, psu = slot('pa')
                    S.op('pe', C('matmul', psu, lhsT=keT[pr, i, :], rhs=vtok[pr, i, :], start=True, stop=True), reads=['keT', VK], writes=[s3])
                    S.op('act', C('activation', out=Ue4[q % 4], in_=psu, func=AF.Identity, scale=etm[:, ci:ci + 1]), reads=[s3, 'etm'], writes=['Ue%d' % (q % 4)])

                def u1(q, ci):
                    cs_, ns_ = q % 4, (q + 1) % 4
                    S.op('dve', C('scalar_tensor_tensor', out=Sst2[ns_], in0=Sst2[cs_], scalar=dec[:, ci:ci + 1], in1=Ue4[q % 4], op0=ALU.mult, op1=ALU.add),
                         reads=['Sst%d' % cs_, 'dec', 'Ue%d' % (q % 4)], writes=['Sst%d' % ns_])
                    nx = corder[q + 1]
                    S.op('act', C('activation', out=Sball[:, nx, :], in_=Sst2[ns_], func=AF.Identity, scale=em[:, nx:nx + 1]), reads=['Sst%d' % ns_, 'em'], writes=['Sb%d' % nx])
                S.tag = 'p2.pass1.a%dd%d' % (a, d)
                pipeline(corder[:-1], [u0, u1], skew=2)
                S.tag = 'p2.pass2.a%dd%d' % (a, d)
                st1 = {}

                def stage1(i):
                    tsl = slice(i * 128, (i + 1) * 128)
                    s1, pss = slot('pb')
                    S.op('pe', C('matmul', pss, lhsT=ke[:, tsl], rhs=qe[:, tsl], start=True, stop=True), reads=['ke', 'qe'], writes=[s1])
                    sm = i % 4
                    S.op('dve', C('tensor_tensor', out=scm[sm], in0=pss, in1=HMASK[d], op=ALU.mult), reads=[s1, 'cst'], writes=['scm%d' % sm])

                def stage2(i):
                    tsl = slice(i * 128, (i + 1) * 128)
                    sm = i % 4
                    s2, pso = slot('pc')
                    items = [C('matmul', pso, lhsT=vtok[:, i, :], rhs=scm[sm], start=True, stop=False)]
                    for c in (0, 1):
                        tk = slice(i * 128 + c * 64, i * 128 + (c + 1) * 64)
                        items.append(C('matmul', pso[:, c * 64:(c + 1) * 64], lhsT=Sball[:, i * 2 + c, :], rhs=qe[:, tk], start=False, stop=(c == 1)))
                    S.op('pe', sum(items, []), reads=[VK, 'scm%d' % sm, 'Sb%d' % (i * 2), 'Sb%d' % (i * 2 + 1), 'qe'], writes=[s2])
                    if d == 0:
                        S.op('act', C('copy', out=oacc[:, tsl], in_=pso), reads=[s2], writes=['oacc%d' % i])
                    else:
                        S.op('dve', C('tensor_tensor', out=oacc[:, tsl], in0=oacc[:, tsl], in1=pso, op=ALU.add), reads=[s2, 'oacc%d' % i], writes=['oacc%d' % i])
                pipeline(list(range(NT)), [lambda k, i: stage1(i), lambda k, i: stage2(i)], skew=3)
            S.tag = 'p2.fin.a%d' % a
            wk = load_w(w_in[l, :, C_G + a * 128:C_G + (a + 1) * 128])

            def f0(k, tg):
                t0, t1 = tg
                n = t1 - t0
                b = k % 2
                okeys = ['oacc%d' % ti for ti in range(t0 // 128, t1 // 128)]
                pk, ps = inproj_fm(l, None, t0, t1, wk)
                S.op('act', C('activation', out=gs[b][:, 0:n], in_=ps, func=AF.Silu), reads=[pk], writes=['gs%d' % b])
                S.op('act', C('activation', out=r1[b][:, 0:n], in_=oacc[:, t0:t1], func=AF.Square), reads=okeys, writes=['r1%d' % b])
                S.op('pe', C('matmul', PC[b][:, 0:n], lhsT=onesF, rhs=r1[b][:, 0:n], start=True, stop=True), reads=['r1%d' % b, 'cst'], writes=['PC%d' % b])

            def f1(k, tg):
                t0, t1 = tg
                n = t1 - t0
                b = k % 2
                okeys = ['oacc%d' % ti for ti in range(t0 // 128, t1 // 128)]
                S.op('act', C('activation', out=r1[b][:, 0:n], in_=PC[b][:, 0:n], func=AF.Ln, scale=1.0 / 128, bias=epsR[:, 0:1]), reads=['PC%d' % b, 'epsR'], writes=['r1%d' % b])
                S.op('act', C('activation', out=r1[b][:, 0:n], in_=r1[b][:, 0:n], func=AF.Exp, scale=-0.5), reads=['r1%d' % b], writes=['r1%d' % b])
                S.op('dve', C('scalar_tensor_tensor', out=r1[b][:, 0:n], in0=oacc[:, t0:t1], scalar=pc(PC_ANW + l), in1=r1[b][:, 0:n], op0=ALU.mult, op1=ALU.mult),
                     reads=okeys + ['r1%d' % b, 'pcol'], writes=['r1%d' % b])
                S.op('dve', C('tensor_tensor', out=yaT[:, a, t0:t1], in0=r1[b][:, 0:n], in1=gs[b][:, 0:n], op=ALU.mult), reads=['r1%d' % b, 'gs%d' % b], writes=['yaT'])
            pipeline(TG, [f0, f1])
        if l == 0:
            dump("yaT", yaT[:], ['yaT'])

        newphase()
        dtT = carve([NT, 32], F32)
        dahi = carve([NT, 32], BF16)
        dalo = carve([NT, 32], BF16)
        acc_off = cur[0]
        acc2 = [carve([512], F32), carve([512], F32)]
        cumT = big[:, acc_off:acc_off + NT * 32].rearrange("p (a b) -> p a b", b=32)
        ncumT = carve([NT, 32], F32)
        decS = carve([NT, 32], F32)
        edT = carve([NT, 32], F32)
        xsT = carve([2, T], BF16)
        bmT = carve([T], BF16)
        cmT = carve([T], BF16)
        xstok = carve([NT, 256], BF16)
        bctok = carve([NT, 128], BF16)
        yacc = carve([2, T], F32)
        Ss2 = [carve([256], F32) for _ in range(3)]
        Ssball = carve([NT, 256], BF16)
        xdtall = carve([NT, 256], BF16)
        xdd = [carve([256], BF16), carve([256], BF16)]
        Lt = [carve([128], F32) for _ in range(4)]
        sq = big[:, cur[0] - 512:cur[0]]
        SQK = ['Lt0', 'Lt1', 'Lt2', 'Lt3']
        Et = [carve([128], F32) for _ in range(4)]
        sqb = big[:, cur[0] - 512:cur[0]]
        SQ2 = [(sq, SQK), (sqb, ['Et0', 'Et1', 'Et2', 'Et3'])]
        Wt = [carve([128], BF16) for _ in range(4)]
        Vt = [carve([128], BF16) for _ in range(4)]
        y2_off = cur[0]
        y2 = [[carve([256], F32), carve([256], F32)] for _ in range(2)]
        daT = big[:, y2_off:y2_off + NT * 32].rearrange("p (a b) -> p a b", b=32)
        S.tag = 'p3.pro'
        wkey, wt = load_w(w_in[l, :, C_DT:C_DT + 32], ncols=32)
        for i in range(NT):
            S.op('pe', MM([(PD[:, 0:32], hT[:, k, i * 128:(i + 1) * 128], wt[:, k, 0:32]) for k in range(8)]), reads=[wkey, 'hT'], writes=['PD'])
            S.op('dve', C('tensor_tensor', out=dtT[:, i, :], in0=PD[:, 0:32], in1=prowB[:, l * 32:(l + 1) * 32], op=ALU.add), reads=['PD', 'prowB'], writes=['dtT'])
        S.op('act', C('activation', out=dtT[:], in_=dtT[:], func=AF.Exp), reads=['dtT'], writes=['dtT'])
        S.op('act', C('activation', out=dtT[:], in_=dtT[:], func=AF.Ln, bias=1.0), reads=['dtT'], writes=['dtT'])
        S.op('dve', C('tensor_tensor', out=daT[:], in0=dtT[:], in1=anegB[:, l * 32:(l + 1) * 32].unsqueeze(1).broadcast_to([128, NT, 32]), op=ALU.mult),
             reads=['dtT', 'anegB'], writes=['daT'])
        S.op('act', C('copy', out=dahi[:], in_=daT[:]), reads=['daT'], writes=['dahi'])
        S.op('dve', C('tensor_tensor', out=dalo[:], in0=daT[:], in1=dahi[:], op=ALU.subtract), reads=['daT', 'dahi'], writes=['dalo'])
        for i in range(NT):
            S.op('pe', C('matmul', PD[:, 128:144], lhsT=TRI[0], rhs=daT[:, i, 0:16], start=True, stop=True)
                 + C('matmul', PD[:, 144:160], lhsT=TRI[1], rhs=daT[:, i, 16:32], start=True, stop=True)
                 + C('matmul', PD[:, 160:192], lhsT=onesF, rhs=daT[:, i, :], start=True, stop=True),
                 reads=['daT', 'cst'], writes=['PD'])
            S.op('act', C('copy', out=cumT[:, i, :], in_=PD[:, 128:160]), reads=['PD'], writes=['cumT'])
            S.op('act', C('activation', out=decS[:, i, :], in_=PD[:, 160:192], func=AF.Exp), reads=['PD'], writes=['decS'])
            S.op('dve', C('tensor_tensor', out=edT[:, i, :], in0=PD[:, 160:192], in1=cumT[:, i, :], op=ALU.subtract), reads=['PD', 'cumT'], writes=['edT'])
        S.op('act', C('activation', out=edT[:], in_=edT[:], func=AF.Exp), reads=['edT'], writes=['edT'])
        S.op('act', C('mul', out=ncumT[:], in_=cumT[:], mul=-1.0), reads=['cumT'], writes=['ncumT'])

        S.barrier()
        P3 = max([p for p in phases if 30 <= p < 40] + [39 if 3 in phases else 0])
        for g in (range(4) if 3 in phases else ([0] if P3 >= 32 else [])):
            S.tag = 'p3.conv.g%d' % g
            specs = [(C_XS + g * 256, 8 * g // 4 * 0 + (g * 2), xsT[:, 0, :]), (C_XS + g * 256 + 128, g * 2 + 1, xsT[:, 1, :]),
                     (C_BM + g * 128, 8 + g, bmT), (C_CM + g * 128, 12 + g, cmT)]
            citems = []
            for (col0, cidx, dst) in specs:
                wk = load_w(w_in[l, :, col0:col0 + 128])
                for tg in TG:
                    citems.append((wk, cidx, dst, tg))
            cst_ = {}

            def c0(k, it):
                wk, cidx, dst, (t0, t1) = it
                n = t1 - t0
                pk, ps = inproj_fm(l, None, t0, t1, wk)
                cst_[k] = (pk, ps)
                S.op('act', C('activation', out=acc2[k % 2][:, 0:n], in_=ps, func=AF.Identity, scale=pc(PC_CW + (l * 16 + cidx) * 5 + 2), bias=pc(PC_CB + l * 16 + cidx)),
                     reads=[pk, 'pcol'], writes=['acc%d' % (k % 2)])

            def c1(k, it):
                wk, cidx, dst, (t0, t1) = it
                n = t1 - t0
                RW = 256 if t0 == 0 else 64
                pk, ps = cst_.pop(k)
                u3 = ps.rearrange("p (r w) -> p r w", w=RW)
                a3 = acc2[k % 2][:, 0:n].rearrange("p (r w) -> p r w", w=RW)
                for j in (0, 1, 3, 4):
                    sh = j - 2
                    lo, hi = max(0, -sh), RW - max(0, sh)
                    S.op('dve', C('scalar_tensor_tensor', out=a3[:, :, lo:hi], in0=u3[:, :, lo + sh:hi + sh], scalar=pc(PC_CW + (l * 16 + cidx) * 5 + j), in1=a3[:, :, lo:hi],
                                  op0=ALU.mult, op1=ALU.add), reads=[pk, 'acc%d' % (k % 2), 'pcol'], writes=['acc%d' % (k % 2)])
                S.op('act', C('activation', out=dst[:, t0:t1], in_=acc2[k % 2][:, 0:n], func=AF.Silu), reads=['acc%d' % (k % 2)], writes=['xbc'])
            pipeline(citems, [c0, c1])
            S.tag = 'p3.tok.g%d' % g
            for i in (range(NT) if P3 >= 33 else []):
                tsl = slice(i * 128, (i + 1) * 128)
                tk_, tb_ = tbank(i)
                S.op('pe', C('transpose', out=tb_[:, 0:128], in_=xsT[:, 0, tsl], identity=identB[:])
                     + C('transpose', out=tb_[:, 128:256], in_=xsT[:, 1, tsl], identity=identB[:])
                     + C('transpose', out=tb_[:, 256:384], in_=bmT[:, tsl], identity=identB[:]), reads=['xbc', 'identB'], writes=[tk_])
                S.op('act', C('copy', out=xstok[:, i, :], in_=tb_[:, 0:256]), reads=[tk_], writes=['xstok'])
                S.op('dve', C('tensor_copy', out=bctok[:, i, :], in_=tb_[:, 256:384]), reads=[tk_], writes=['bctok'])
            for d in (range(2) if P3 >= 34 else []):
                hd0 = d * 16 + g * 4
                order = list(range(NT)) if d == 0 else [1, 0] + list(range(NT - 1, 1, -1))
                S.op('dve', C('memset', Ss2[0], 0.0), writes=['Ss0'])
                S.op('pool', C('memset', Ssball[:, order[0], :], 0.0), writes=['Ssb%d' % order[0]])

                def chain_a(q):
                    i = order[q]
                    S.op('dve', C('tensor_tensor', out=xdtall[:, i, :].rearrange("p (h q) -> p h q", q=64), in0=xstok[:, i, :].rearrange("p (h q) -> p h q", q=64),
                                  in1=dtT[:, i, hd0:hd0 + 4].unsqueeze(2).broadcast_to([128, 4, 64]), op=ALU.mult),
                         reads=['xstok', 'dtT'], writes=['xdt%d' % i])
                    if q + 1 == len(order):
                        return
                    xb_ = q % 2
                    S.op('dve', C('tensor_tensor', out=xdd[xb_][:, :].rearrange("p (h q) -> p h q", q=64), in0=xdtall[:, i, :].rearrange("p (h q) -> p h q", q=64),
                                  in1=edT[:, i, hd0:hd0 + 4].unsqueeze(2).broadcast_to([128, 4, 64]), op=ALU.mult),
                         reads=['xdt%d' % i, 'edT'], writes=['xdd%d' % xb_])

                def chain_b(q):
                    i = order[q]
                    if q + 1 == len(order):
                        return
                    xb_ = q % 2
                    ukey, ups = 'PA%d' % (q % 2), PA[q % 2][:, 256:512]
                    S.op('pe', C('matmul', ups, lhsT=bctok[:, i, :], rhs=xdd[xb_][:, :], start=True, stop=True), reads=['bctok', 'xdd%d' % xb_], writes=[ukey])

                def chain_c(q):
                    i = order[q]
                    if q + 1 == len(order):
                        return
                    ukey, ups = 'PA%d' % (q % 2), PA[q % 2][:, 256:512]
                    cs_, ns_ = q % 3, (q + 1) % 3
                    S.op('dve', C('tensor_tensor', out=Ss2[ns_][:, :].rearrange("p (h q) -> p h q", q=64), in0=Ss2[cs_][:, :].rearrange("p (h q) -> p h q", q=64),
                                  in1=decS[:, i, hd0:hd0 + 4].unsqueeze(2).broadcast_to([128, 4, 64]), op=ALU.mult),
                         reads=['Ss%d' % cs_, 'decS'], writes=['Ss%d' % ns_])
                    S.op('dve', C('tensor_tensor', out=Ss2[ns_], in0=Ss2[ns_], in1=ups, op=ALU.add), reads=['Ss%d' % ns_, ukey], writes=['Ss%d' % ns_])
                    nx = order[q + 1]
                    S.op('act', C('copy', out=Ssball[:, nx, :], in_=Ss2[ns_]), reads=['Ss%d' % ns_], writes=['Ssb%d' % nx])

                def chain_step(q):
                    chain_a(q)
                    chain_b(q)
                    chain_c(q)
                chain_step(0)
                S.tag = 'p3.pass2.g%dd%d' % (g, d)
                S.tag = 'p3.pass2.g%dd%d' % (g, d)
                jobs = [(i, r) for i in order for r in range(4)]
                qpos = {i: q for q, i in enumerate(order)}
                cbs = {}
                ysl = {}

                def stage1(k, i, r):
                    tsl = slice(i * 128, (i + 1) * 128)
                    hd = hd0 + r
                    if qpos[i] + 1 < len(order):
                        if r == 0:
                            chain_a(qpos[i] + 1)
                        elif r == 1:
                            chain_b(qpos[i] + 1)
                        elif r == 2:
                            chain_c(qpos[i] + 1)
                    if r == 0:
                        sc_ = nxt('pcb', 4)
                        s1, pcb = 'PA%d' % (sc_ % 2), PA[sc_ % 2][:, (sc_ // 2) * 128:(sc_ // 2) * 128 + 128]
                        cbs[i] = (s1, pcb)
                        S.op('pe', C('matmul', pcb, lhsT=bmT[:, tsl], rhs=cmT[:, tsl], start=True, stop=True), reads=['xbc'], writes=[s1])
                    s1, pcb = cbs[i]
                    bk = k % 3
                    bank3 = (PB[0], PB[1], PD)[bk]
                    pA, pBm = bank3[:, 0:128], bank3[:, 128:256]
                    bkey = ('PB0', 'PB1', 'PD')[bk]
                    dh = dahi[:, i, hd:hd + 1].broadcast_to([128, 128])
                    dl = dalo[:, i, hd:hd + 1].broadcast_to([128, 128])
                    S.op('pe', C('matmul', pA, lhsT=dh, rhs=TRIB[d], start=True, stop=False)
                         + C('matmul', pA, lhsT=dl, rhs=TRIB[d], start=False, stop=False)
                         + C('matmul', pA, lhsT=identB[:], rhs=MNEGB[d], start=False, stop=True)
                         + C('matmul', pBm, lhsT=dh, rhs=TRIB[d], start=True, stop=False)
                         + C('matmul', pBm, lhsT=dl, rhs=TRIB[d], start=False, stop=True), reads=['dahi', 'dalo', 'cstB', 'identB'], writes=[bkey])
                    b3 = k % 4
                    S.op('act', C('activation', out=Lt[b3], in_=pA, func=AF.Exp, bias=ncumT[:, i, hd:hd + 1]), reads=[bkey, 'ncumT'], writes=['Lt%d' % b3])
                    S.op('act', C('activation', out=Et[b3], in_=pBm, func=AF.Exp), reads=[bkey], writes=['Et%d' % b3])
                    S.op('dve', C('tensor_tensor', out=Wt[b3], in0=pcb, in1=Lt[b3], op=ALU.mult), reads=[s1, 'Lt%d' % b3], writes=['Wt%d' % b3])
                    S.op('pool', C('tensor_tensor', out=Vt[b3], in0=Et[b3], in1=cmT[:, tsl], op=ALU.mult), reads=['Et%d' % b3, 'xbc'], writes=['Vt%d' % b3])

                def stage2(k, i, r):
                    tsl = slice(i * 128, (i + 1) * 128)
                    j, r2 = r // 2, r % 2
                    if r2 == 0:
                        ysl[(i, j)] = slot('pc')
                    s2, psy = ysl[(i, j)]
                    b3 = k % 4
                    pr = slice(r2 * 64, (r2 + 1) * 64)
                    cs = slice(r * 64, (r + 1) * 64)
                    S.op('pe', C('matmul', psy[pr, :], lhsT=xdtall[:, i, cs], rhs=Wt[b3], start=True, stop=False)
                         + C('matmul', psy[pr, :], lhsT=Ssball[:, i, cs], rhs=Vt[b3], start=False, stop=True),
                         reads=['xdt%d' % i, 'Wt%d' % b3, 'Vt%d' % b3, 'Ssb%d' % i], writes=[s2])
                    if r2 == 1:
                        yk = 'ya%d_%d' % (j, i)
                        if d == 0:
                            S.op('dve', C('scalar_tensor_tensor', out=yacc[:, j, tsl], in0=xsT[:, j, tsl], scalar=pc(PC_DSK + l * 8 + g * 2 + j), in1=psy, op0=ALU.mult, op1=ALU.add),
                                 reads=[s2, 'xbc', 'pcol'], writes=[yk])
                        else:
                            S.op('dve', C('tensor_tensor', out=yacc[:, j, tsl], in0=yacc[:, j, tsl], in1=psy, op=ALU.add), reads=[s2, yk], writes=[yk])
                pipeline(jobs, [lambda k, j: stage1(k, *j), lambda k, j: stage2(k, *j)], skew=3)
            S.tag = 'p3.fin.g%d' % g
            wz = [load_w(w_in[l, :, C_Z + g * 256 + j * 128:C_Z + g * 256 + (j + 1) * 128]) for j in range(2)]
            TG2 = [(q * 256, (q + 1) * 256) for q in range(T // 256)] if P3 >= 35 else []

            def g0(k, tg):
                t0, t1 = tg
                n = t1 - t0
                b = k % 2
                sqt, sqk = SQ2[b]
                pks = [inproj_fm(l, None, t0, t1, wz[j]) for j in range(2)]
                for j in range(2):
                    pk, ps = pks[j]
                    zb = acc2[j]
                    S.op('act', C('activation', out=zb[:, 0:n], in_=ps, func=AF.Silu), reads=[pk], writes=['acc%d' % j])
                    S.op('dve', C('tensor_tensor', out=y2[j][b][:, 0:n], in0=yacc[:, j, t0:t1], in1=zb[:, 0:n], op=ALU.mult),
                         reads=['acc%d' % j] + ['ya%d_%d' % (j, ti) for ti in range(t0 // 128, t1 // 128)], writes=['y2%d%d' % (j, b)])
                for j in range(2):
                    S.op('act', C('activation', out=sqt[:, j * 256:j * 256 + n], in_=y2[j][b][:, 0:n], func=AF.Square), reads=['y2%d%d' % (j, b)], writes=sqk)
                S.op('pe', C('matmul', PC[b][:, 0:n], lhsT=onesF, rhs=sqt[:, 0:n], start=True, stop=False)
                     + C('matmul', PC[b][:, 0:n], lhsT=onesF, rhs=sqt[:, 256:256 + n], start=False, stop=True), reads=sqk + ['cst'], writes=['PC%d' % b])

            def g1(k, tg):
                t0, t1 = tg
                n = t1 - t0
                b = k % 2
                sqt, sqk = SQ2[b]
                S.op('act', C('activation', out=sqt[:, 0:n], in_=PC[b][:, 0:n], func=AF.Ln, scale=1.0 / 256, bias=epsR[:, 0:1]), reads=['PC%d' % b, 'epsR'], writes=sqk)
                S.op('act', C('activation', out=sqt[:, 0:n], in_=sqt[:, 0:n], func=AF.Exp, scale=-0.5), reads=sqk, writes=sqk)
                for j in range(2):
                    S.op('dve', C('scalar_tensor_tensor', out=ybT[:, g * 2 + j, t0:t1], in0=y2[j][b][:, 0:n], scalar=pc(PC_BNW + l * 8 + g * 2 + j),
                                  in1=sqt[:, 0:n], op0=ALU.mult, op1=ALU.mult), reads=['y2%d%d' % (j, b), 'pcol'] + sqk, writes=['ybT'])
            S.tag = 'p3.fin.g%d' % g
            pipeline(TG2, [g0, g1])
        if l == 0:
            dump("ybT", ybT[:], ['ybT'])

        newphase()
        S.tag = 'p4.a'
        mg = carve([8, T], BF16)
        wo_sb = carve([8, D], BF16)
        wpa_c = carve([4, 128], BF16)
        wpb_c = carve([8, 128], BF16)
        sa = carve([512], F32)
        sb_ = carve([512], F32)
        gB = [carve([D], F32), carve([D], F32)]
        lnG = carve([D], F32)
        lnB = carve([D], F32)
        xt4 = [carve([D], F32), carve([D], F32)]
        vv = [carve([D], F32), carve([D], F32)]
        st6 = [carve([2, 6], F32), carve([2, 6], F32)]
        mv = [carve([4], F32), carve([4], F32)]
        for k in range(8):
            for hf in range(2):
                S.dma('pool', C('dma_start', out=wo_sb[:, k, hf * 512:(hf + 1) * 512], in_=w_o[l, k * 128:(k + 1) * 128, hf * 512:(hf + 1) * 512]), writes=['wo_sb'])
        S.dma('sp', C('dma_start', out=lnG, in_=ln_g[l:l + 1, :].partition_broadcast(128)), writes=['lnG'])
        S.dma('sp', C('dma_start', out=lnB, in_=ln_b[l:l + 1, :].partition_broadcast(128)), writes=['lnB'])
        for w in range(2):
            bcast_tile(gB[w], 'gB%d' % w, 2, w)
        for dc in range(8):
            wga = load_w(w_in[l, :, C_GA + dc * 128:C_GA + (dc + 1) * 128])
            wgb = load_w(w_in[l, :, C_GB + dc * 128:C_GB + (dc + 1) * 128])
            S.dma('pool', C('dma_start', out=wpa_c, in_=w_pa[l, :, dc * 128:(dc + 1) * 128].rearrange("(k p) c -> p k c", p=128)), writes=['wpa_c'])
            S.dma('pool', C('dma_start', out=wpb_c, in_=w_pb[l, :, dc * 128:(dc + 1) * 128].rearrange("(k p) c -> p k c", p=128)), writes=['wpb_c'])
            for (t0, t1) in TG:
                n = t1 - t0
                pk, ps = inproj_fm(l, None, t0, t1, wga)
                S.op('act', C('activation', out=sa[:, 0:n], in_=ps, func=AF.Sigmoid), reads=[pk], writes=['sa'])
                pk, ps = inproj_fm(l, None, t0, t1, wgb)
                S.op('act', C('activation', out=sb_[:, 0:n], in_=ps, func=AF.Sigmoid), reads=[pk], writes=['sb'])
                S.op('pe', MM([(PC[0][:, 0:n], wpa_c[:, k, :], yaT[:, k, t0:t1]) for k in range(4)]), reads=['wpa_c', 'yaT'], writes=['PC0'])
                S.op('pe', MM([(PC[1][:, 0:n], wpb_c[:, k, :], ybT[:, k, t0:t1]) for k in range(8)]), reads=['wpb_c', 'ybT'], writes=['PC1'])
                S.op('dve', C('tensor_tensor', out=sa[:, 0:n], in0=PC[0][:, 0:n], in1=sa[:, 0:n], op=ALU.mult), reads=['sa', 'PC0'], writes=['sa'])
                S.op('dve', C('tensor_tensor', out=sb_[:, 0:n], in0=PC[1][:, 0:n], in1=sb_[:, 0:n], op=ALU.mult), reads=['sb', 'PC1'], writes=['sb'])
                S.op('pool', C('tensor_tensor', out=mg[:, dc, t0:t1], in0=sa[:, 0:n], in1=sb_[:, 0:n], op=ALU.add), reads=['sa', 'sb'], writes=['mg'])
        if l + 1 < nlayers:
            emit_mod(l + 1)
        S.tag = 'p4.b'
        tiles4 = [i for i in range(NT) if not (last and i < 2)]

        def b0(k, i):
            b = k % 2
            w = 1 if i < 2 else 0
            tsl = slice(i * 128, (i + 1) * 128)
            if l == 0:
                src_ = ctx_in[i * 128:(i + 1) * 128, :] if i < 2 else x_in[(i - 2) * 128:(i - 1) * 128, :]
            else:
                src_ = xres[i * 128:(i + 1) * 128, :]
            S.dma('sp', C('dma_start', out=xt4[b], in_=src_), reads=['xres%d' % i], writes=['xt4%d' % b])
            banks = (PC, PB)[b]
            bkeys = (('PC0', 'PC1'), ('PB0', 'PB1'))[b]
            for hf in range(2):
                S.op('pe', MM([(banks[hf][:, :], mg[:, kk_, tsl], wo_sb[:, kk_, hf * 512:(hf + 1) * 512]) for kk_ in range(8)]),
                     reads=['mg', 'wo_sb'], writes=[bkeys[hf]])
                S.op('dve', C('tensor_tensor', out=vv[b][:, hf * 512:(hf + 1) * 512], in0=banks[hf][:, :], in1=gB[w][:, hf * 512:(hf + 1) * 512], op=ALU.mult),
                     reads=['gB%d' % w, bkeys[hf]], writes=['vv%d' % b])

        def b1(k, i):
            b = k % 2
            vk, mk, sk = 'vv%d' % b, 'mv%d' % b, 'st6%d' % b
            v_, m_, s_ = vv[b], mv[b], st6[b]
            S.op('dve', C('scalar_tensor_tensor', out=v_, in0=xt4[b], scalar=float(DN_ALPHA), in1=v_, op0=ALU.mult, op1=ALU.add), reads=['xt4%d' % b, vk], writes=[vk])
            S.op('dve', C('bn_stats', out=s_[:, 0, :], in_=v_[:, 0:512]), reads=[vk], writes=[sk])
            S.op('dve', C('bn_stats', out=s_[:, 1, :], in_=v_[:, 512:1024]), reads=[vk], writes=[sk])
            S.op('dve', C('bn_aggr', out=m_[:, 0:2], in_=s_[:, :, :]), reads=[sk], writes=[mk])
            S.op('act', C('activation', out=m_[:, 2:3], in_=m_[:, 1:2], func=AF.Ln, bias=epsR[:, 1:2]), reads=[mk, 'epsR'], writes=[mk])
            S.op('act', C('activation', out=m_[:, 2:3], in_=m_[:, 2:3], func=AF.Exp, scale=-0.5), reads=[mk], writes=[mk])
            S.op('dve', C('scalar_tensor_tensor', out=m_[:, 3:4], in0=m_[:, 0:1], scalar=-1.0, in1=m_[:, 2:3], op0=ALU.mult, op1=ALU.mult), reads=[mk], writes=[mk])
            S.op('act', C('activation', out=v_, in_=v_, func=AF.Identity, scale=m_[:, 2:3], bias=m_[:, 3:4]), reads=[vk, mk], writes=[vk])
            S.op('dve', C('tensor_tensor', out=v_, in0=v_, in1=lnG, op=ALU.mult), reads=[vk, 'lnG'], writes=[vk])
            S.op('pool', C('tensor_tensor', out=xt4[b], in0=v_, in1=lnB, op=ALU.add), reads=[vk, 'lnB', 'xt4%d' % b], writes=['xt4%d' % b])
            dst = out[(i - 2) * 128:(i - 1) * 128, :] if last else xres[i * 128:(i + 1) * 128, :]
            S.dma('sp', C('dma_start', out=dst, in_=xt4[b]), reads=['xt4%d' % b], writes=['xres%d' % i])
        pipeline(tiles4, [b0, b1])
    if dbg and nlayers < DEPTH:
        S.barrier()
        dres = nc.dram_tensor("dbg_xres", [T, D], F32, kind="ExternalOutput").ap()
        dbg_outs["xres"] = [T, D]
        S.dma('sp', C('dma_start', out=dres, in_=xres), reads=['xres%d' % i for i in range(NT)])
    with nc.Block() as block:
        S.emit(block)
    return nc, dbg_outs


def make_consts():
    c = np.zeros((128, K_N), np.float32)
    s = np.arange(128)[:, None]
    t = np.arange(128)[None, :]
    c[:, K_ID:K_ID + 128] = np.eye(128)
    c[:, K_ONE:K_ONE + 128] = 1.0
    c[:, K_TRIF:K_TRIF + 128] = (s <= t)
    c[:, K_TRIB:K_TRIB + 128] = (s >= t)
    c[:, K_MNF:K_MNF + 128] = np.where(s <= t, 0.0, NEG)
    c[:, K_MNB:K_MNB + 128] = np.where(s >= t, 0.0, NEG)
    same = (s // 64) == (t // 64)
    c[:, K_HMF:K_HMF + 128] = (s <= t) & same
    c[:, K_HMB:K_HMB + 128] = (s >= t) & same
    rm = np.ones(512, np.float32)
    rm[::64] = 0.0
    c[:, K_RM:K_RM + 512] = rm[None, :]
    return c


def pack_cols(inp, b):
    pcv = np.zeros((128, PC_N), np.float32)
    lbl = np.asarray(inp['a_lb_logits'], np.float32).reshape(2, 4, 4, 128)
    pcv[:, PC_LBL:PC_LBL + 32] = lbl.transpose(3, 0, 1, 2).reshape(128, 32)
    pcv[:, PC_ANW:PC_ANW + 4] = np.asarray(inp['a_norm_w'], np.float32).T
    cw = np.asarray(inp['b_conv_w'], np.float32).reshape(4, 5, 16, 128)
    pcv[:, PC_CW:PC_CW + 320] = cw.transpose(3, 0, 2, 1).reshape(128, 320)
    cb = np.asarray(inp['b_conv_b'], np.float32).reshape(4, 16, 128)
    pcv[:, PC_CB:PC_CB + 64] = cb.transpose(2, 0, 1).reshape(128, 64)
    dsk = np.repeat(np.asarray(inp['b_d'], np.float32), 64, axis=1).reshape(4, 8, 128)
    pcv[:, PC_DSK:PC_DSK + 32] = dsk.transpose(2, 0, 1).reshape(128, 32)
    bnw = np.asarray(inp['b_norm_w'], np.float32).reshape(4, 8, 128)
    pcv[:, PC_BNW:PC_BNW + 32] = bnw.transpose(2, 0, 1).reshape(128, 32)
    bm = np.asarray(inp['b_mod'], np.float32).reshape(4, 24, 128)
    pcv[:, PC_BMOD:PC_BMOD + 96] = bm.transpose(2, 0, 1).reshape(128, 96)
    pcv[:, PC_C:PC_C + 8] = np.asarray(inp['c'], np.float32)[b].reshape(8, 128).T
    pcv[:, PC_C + 8:PC_C + 16] = np.asarray(inp['c_ctx'], np.float32).reshape(8, 128).T
    prow = np.concatenate([np.asarray(inp['b_dt_bias'], np.float32).reshape(-1), np.asarray(inp['b_a_log'], np.float32).reshape(-1)])[None, :]
    return pcv, np.ascontiguousarray(prow)


_CACHE = {}


def kernel(**inputs):
    if 'nc' not in _CACHE:
        _CACHE['nc'] = build()[0]
    nc = _CACHE['nc']
    cstv = make_consts()
    f = lambda k: np.ascontiguousarray(np.asarray(inputs[k], np.float32))
    shared = {k: f(k) for k in ['w_mod', 'w_in', 'w_proj_a', 'w_proj_b', 'w_out', 'ln_g', 'ln_b']}
    x = f('x')
    ctx = f('ctx')
    in_maps = []
    for b in range(8):
        pcv, prow = pack_cols(inputs, b)
        m = dict(shared)
        m.update({'x': x[b], 'ctx': ctx[b], 'pcol_d': pcv, 'prow_d': prow, 'cst_d': cstv})
        in_maps.append(m)
    res = run_bass_kernel_spmd(nc, in_maps, core_ids=list(range(8)))
    return np.stack([np.asarray(r['out'], np.float32) for r in res.results], axis=0)
```
